# Optimizing a Trainium2 kernel written in Bass

```python
import jax, jax.numpy as jnp
from jax import lax
import numpy as np

D_MODEL = 1024
BATCH = 8
SEQ = 2048
DEPTH = 2
DEC_BATCH = 128
DEC_SEQ = 1
PAST_LEN = 16384
PAGE_SIZE = 128

N_MIXERS = 4
D_MIX = D_MODEL
D_GROUP = D_MIX // N_MIXERS
D_IN_PROJ = 8 * D_GROUP
GMLP_HEADS = 4
GMLP_HEAD_DIM = D_GROUP // GMLP_HEADS
CHUNK = 128
CONF_WIDTH = 31
SC_WIDTH = 3
POOL_WINDOWS = (2, 4, 8, 16)
POOL_GROUPS = len(POOL_WINDOWS)
POOL_GROUP_DIM = D_GROUP // POOL_GROUPS
POOL_BUF = max(POOL_WINDOWS) - 1
MEM_LEN = 256
XATTN_HEADS = 4
XATTN_HEAD_DIM = D_MODEL // XATTN_HEADS
D_FF = 4 * D_MODEL
EPS = 1e-6

kernel_name = 'hybrid_headgroup_decoder_step'


def rms_norm(x, g):
    xf = x.astype(jnp.float32)
    y = xf * lax.rsqrt(jnp.mean(xf * xf, axis=-1, keepdims=True) + EPS)
    return (y * g.astype(jnp.float32)).astype(x.dtype)


def layer_norm(x, g, b):
    xf = x.astype(jnp.float32)
    xc = xf - jnp.mean(xf, axis=-1, keepdims=True)
    y = xc * lax.rsqrt(jnp.mean(xc * xc, axis=-1, keepdims=True) + EPS)
    return (y * g.astype(jnp.float32) + b.astype(jnp.float32)).astype(x.dtype)


def causal_depthwise_conv(buf, x, w):
    xx = jnp.concatenate([buf.astype(x.dtype), x], axis=1)
    out = lax.conv_general_dilated(xx, w[:, None, :].astype(x.dtype), window_strides=(1,), padding='VALID',
                                   dimension_numbers=('NWC', 'WIO', 'NWC'), feature_group_count=x.shape[-1])
    return out, xx[:, -(w.shape[0] - 1):]


def chunk_spatial_gate(v, ws, bs):
    bsz, t, c = v.shape
    n_chunks = -(-t // CHUNK)
    vp = jnp.pad(v, ((0, 0), (0, n_chunks * CHUNK - t), (0, 0)))
    vp = vp.reshape(bsz, n_chunks, CHUNK, GMLP_HEADS, GMLP_HEAD_DIM)
    mask = jnp.tril(jnp.ones((CHUNK, CHUNK), dtype=bool))
    wsm = jnp.where(mask[None], ws, jnp.zeros_like(ws)).astype(v.dtype)
    z = jnp.einsum('hts,bcshd->bcthd', wsm, vp) + bs.T.astype(v.dtype)[None, None, :, :, None]
    return z.reshape(bsz, n_chunks * CHUNK, c)[:, :t]


def multiscale_pool(buf, x, pos0, w_pool, scale):
    bsz, t, c = x.shape
    xx = jnp.concatenate([buf.astype(x.dtype), x], axis=1)
    cs = jnp.cumsum(xx.astype(jnp.float32), axis=1)
    cs = jnp.concatenate([jnp.zeros((bsz, 1, c), jnp.float32), cs], axis=1)
    ends = cs[:, POOL_BUF + 1:POOL_BUF + 1 + t]
    pos = pos0 + jnp.arange(t, dtype=jnp.int32)
    outs = []
    for g, w in enumerate(POOL_WINDOWS):
        lo, hi = g * POOL_GROUP_DIM, (g + 1) * POOL_GROUP_DIM
        starts = cs[:, POOL_BUF + 1 - w:POOL_BUF + 1 - w + t, lo:hi]
        cnt = jnp.minimum(w, pos + 1).astype(jnp.float32)[None, :, None]
        outs.append((ends[..., lo:hi] - starts) / cnt)
    pooled = (jnp.concatenate(outs, axis=-1) - x.astype(jnp.float32)).astype(x.dtype)
    pooled = pooled.reshape(bsz, t, POOL_GROUPS, POOL_GROUP_DIM)
    y = jnp.einsum('btgc,gcd->btgd', pooled, w_pool).reshape(bsz, t, c) * scale
    return y, xx[:, -POOL_BUF:]


def token_mixers(h, buf_glu, buf_short, buf_pool, pos0, w_in, gmlp_ln_g, gmlp_ln_b, gmlp_ws, gmlp_bs,
                 conf_dw, conf_dw_b, conf_ln_g, conf_ln_b, sc_dw, pool_w, pool_scale, mix_out_g, w_out):
    bsz, t, _ = h.shape
    z = h @ w_in
    u, v, glu_a, glu_g, sc_b, sc_c, sc_x, pool_x = jnp.split(z, 8, axis=-1)
    vn = layer_norm(v, gmlp_ln_g, gmlp_ln_b)
    y_a = u * chunk_spatial_gate(vn, gmlp_ws, gmlp_bs)
    glu = glu_a * jax.nn.sigmoid(glu_g)
    conv_b, new_glu = causal_depthwise_conv(buf_glu, glu, conf_dw)
    y_b = jax.nn.silu(layer_norm(conv_b + conf_dw_b, conf_ln_g, conf_ln_b))
    conv_c, new_short = causal_depthwise_conv(buf_short, sc_c * sc_x, sc_dw)
    y_c = sc_b * conv_c
    y_d, new_pool = multiscale_pool(buf_pool, pool_x, pos0, pool_w, pool_scale)
    y = jnp.stack([y_a, y_b, y_c, y_d], axis=2)
    y = rms_norm(y, mix_out_g.reshape(N_MIXERS, D_GROUP)).reshape(bsz, t, D_MIX)
    return y @ w_out, vn, new_glu, new_short, new_pool


def mem_kv(mem, g_mem, w_k, w_v):
    bsz = mem.shape[0]
    m = rms_norm(mem, g_mem)
    k = (m @ w_k).reshape(bsz, MEM_LEN, XATTN_HEADS, XATTN_HEAD_DIM)
    v = (m @ w_v).reshape(bsz, MEM_LEN, XATTN_HEADS, XATTN_HEAD_DIM)
    return k, v


def cross_attend(h, k, v, w_q, w_o):
    bsz, t, _ = h.shape
    q = (h @ w_q).reshape(bsz, t, XATTN_HEADS, XATTN_HEAD_DIM)
    s = jnp.einsum('bthd,bmhd->bhtm', q, k.astype(h.dtype)).astype(jnp.float32) * (XATTN_HEAD_DIM ** -0.5)
    p = jax.nn.softmax(s, axis=-1).astype(h.dtype)
    o = jnp.einsum('bhtm,bmhd->bthd', p, v.astype(h.dtype)).reshape(bsz, t, XATTN_HEADS * XATTN_HEAD_DIM)
    return o @ w_o


def sq_relu_mlp(h, w1, w2):
    a = jax.nn.relu(h @ w1)
    return (a * a) @ w2


def nrm(k, shape, scale):
    return jax.random.normal(k, shape, jnp.float32) * scale


def gain(k, shape):
    return 1.0 + 0.02 * jax.random.normal(k, shape, jnp.float32)


def setup_inputs(seed: int = 0) -> dict:
    key = jax.random.key(seed)
    ks = jax.random.split(key, 33)
    hd = XATTN_HEADS * XATTN_HEAD_DIM
    return {
        'x_prompt': nrm(ks[0], (BATCH, SEQ, D_MODEL), 1.0),
        'x_sample': nrm(ks[1], (DEC_BATCH, DEC_SEQ, D_MODEL), 1.0),
        'mem_prompt': nrm(ks[2], (BATCH, MEM_LEN, D_MODEL), 1.0),
        'cache_mem_k': nrm(ks[3], (DEPTH, DEC_BATCH, MEM_LEN, XATTN_HEADS, XATTN_HEAD_DIM), 1.0),
        'cache_mem_v': nrm(ks[4], (DEPTH, DEC_BATCH, MEM_LEN, XATTN_HEADS, XATTN_HEAD_DIM), 1.0),
        'state_conv_glu': nrm(ks[5], (DEPTH, DEC_BATCH, CONF_WIDTH - 1, D_GROUP), 0.5),
        'state_conv_short': nrm(ks[6], (DEPTH, DEC_BATCH, SC_WIDTH - 1, D_GROUP), 0.5),
        'state_pool': nrm(ks[7], (DEPTH, DEC_BATCH, POOL_BUF, D_GROUP), 1.0),
        'norm_mix': gain(ks[8], (DEPTH, D_MODEL)),
        'w_in': nrm(ks[9], (DEPTH, D_MODEL, D_IN_PROJ), D_MODEL ** -0.5),
        'gmlp_ln_g': gain(ks[10], (DEPTH, D_GROUP)),
        'gmlp_ln_b': nrm(ks[11], (DEPTH, D_GROUP), 0.02),
        'gmlp_ws': nrm(ks[12], (DEPTH, GMLP_HEADS, CHUNK, CHUNK), 0.5 * CHUNK ** -0.5),
        'gmlp_bs': gain(ks[13], (DEPTH, GMLP_HEADS, CHUNK)),
        'conf_dw': nrm(ks[14], (DEPTH, CONF_WIDTH, D_GROUP), CONF_WIDTH ** -0.5),
        'conf_dw_b': nrm(ks[15], (DEPTH, D_GROUP), 0.02),
        'conf_ln_g': gain(ks[16], (DEPTH, D_GROUP)),
        'conf_ln_b': nrm(ks[17], (DEPTH, D_GROUP), 0.02),
        'sc_dw': nrm(ks[18], (DEPTH, SC_WIDTH, D_GROUP), SC_WIDTH ** -0.5),
        'pool_w': nrm(ks[19], (DEPTH, POOL_GROUPS, POOL_GROUP_DIM, POOL_GROUP_DIM), POOL_GROUP_DIM ** -0.5),
        'pool_scale': gain(ks[20], (DEPTH, D_GROUP)),
        'mix_out_g': gain(ks[21], (DEPTH, D_MIX)),
        'w_out': nrm(ks[22], (DEPTH, D_MIX, D_MODEL), D_MIX ** -0.5),
        'norm_xattn': gain(ks[23], (DEPTH, D_MODEL)),
        'norm_mem': gain(ks[24], (DEPTH, D_MODEL)),
        'w_xq': nrm(ks[25], (DEPTH, D_MODEL, hd), D_MODEL ** -0.5),
        'w_xk': nrm(ks[26], (DEPTH, D_MODEL, hd), D_MODEL ** -0.5),
        'w_xv': nrm(ks[27], (DEPTH, D_MODEL, hd), D_MODEL ** -0.5),
        'w_xo': nrm(ks[28], (DEPTH, hd, D_MODEL), hd ** -0.5),
        'norm_ffn': gain(ks[29], (DEPTH, D_MODEL)),
        'w_ff1': nrm(ks[30], (DEPTH, D_MODEL, D_FF), D_MODEL ** -0.5),
        'w_ff2': nrm(ks[31], (DEPTH, D_FF, D_MODEL), D_FF ** -0.5),
        'norm_final': gain(ks[32], (D_MODEL,)),
    }


def reference(x_prompt, x_sample, mem_prompt, cache_mem_k, cache_mem_v, state_conv_glu, state_conv_short,
              state_pool, norm_mix, w_in, gmlp_ln_g, gmlp_ln_b, gmlp_ws, gmlp_bs, conf_dw, conf_dw_b,
              conf_ln_g, conf_ln_b, sc_dw, pool_w, pool_scale, mix_out_g, w_out, norm_xattn, norm_mem,
              w_xq, w_xk, w_xv, w_xo, norm_ffn, w_ff1, w_ff2, norm_final):
    xp, xs = x_prompt, x_sample
    bp = xp.shape[0]
    mk_p, mv_p, glu_p, glu_s, sh_p, sh_s, pl_p, pl_s, v_s = [], [], [], [], [], [], [], [], []
    for l in range(DEPTH):
        mix_l = (w_in[l], gmlp_ln_g[l], gmlp_ln_b[l], gmlp_ws[l], gmlp_bs[l], conf_dw[l], conf_dw_b[l],
                 conf_ln_g[l], conf_ln_b[l], sc_dw[l], pool_w[l], pool_scale[l], mix_out_g[l], w_out[l])
        zb_glu = jnp.zeros((bp, CONF_WIDTH - 1, D_GROUP), xp.dtype)
        zb_sh = jnp.zeros((bp, SC_WIDTH - 1, D_GROUP), xp.dtype)
        zb_pl = jnp.zeros((bp, POOL_BUF, D_GROUP), xp.dtype)
        m_out, _, ng, nsh, npl = token_mixers(rms_norm(xp, norm_mix[l]), zb_glu, zb_sh, zb_pl, 0, *mix_l)
        xp = xp + m_out
        k_p, v_p = mem_kv(mem_prompt, norm_mem[l], w_xk[l], w_xv[l])
        xp = xp + cross_attend(rms_norm(xp, norm_xattn[l]), k_p, v_p, w_xq[l], w_xo[l])
        xp = xp + sq_relu_mlp(rms_norm(xp, norm_ffn[l]), w_ff1[l], w_ff2[l])
        mk_p.append(k_p); mv_p.append(v_p); glu_p.append(ng); sh_p.append(nsh); pl_p.append(npl)
        m_out, vn_s, ng, nsh, npl = token_mixers(rms_norm(xs, norm_mix[l]), state_conv_glu[l],
                                                 state_conv_short[l], state_pool[l], PAST_LEN, *mix_l)
        xs = xs + m_out
        xs = xs + cross_attend(rms_norm(xs, norm_xattn[l]), cache_mem_k[l], cache_mem_v[l], w_xq[l], w_xo[l])
        xs = xs + sq_relu_mlp(rms_norm(xs, norm_ffn[l]), w_ff1[l], w_ff2[l])
        glu_s.append(ng); sh_s.append(nsh); pl_s.append(npl); v_s.append(vn_s)
    y_prompt = rms_norm(xp, norm_final)
    y_sample = rms_norm(xs, norm_final)
    new_mem_k_prompt = jnp.stack(mk_p, axis=0)
    new_mem_v_prompt = jnp.stack(mv_p, axis=0)
    new_conv_glu_prompt = jnp.stack(glu_p, axis=0)
    new_conv_glu_sample = jnp.stack(glu_s, axis=0)
    new_conv_short_prompt = jnp.stack(sh_p, axis=0)
    new_conv_short_sample = jnp.stack(sh_s, axis=0)
    new_pool_prompt = jnp.stack(pl_p, axis=0)
    new_pool_sample = jnp.stack(pl_s, axis=0)
    new_gmlp_v_sample = jnp.stack(v_s, axis=0)
    return (y_prompt, y_sample, new_mem_k_prompt, new_mem_v_prompt, new_conv_glu_prompt, new_conv_glu_sample,
            new_conv_short_prompt, new_conv_short_sample, new_pool_prompt, new_pool_sample, new_gmlp_v_sample)
```

```python
import contextlib
import os
import numpy as np
import concourse.bass as bass
import concourse.mybir as mybir
from concourse.bass_utils import run_bass_kernel_spmd

F32 = mybir.dt.float32
BF16 = mybir.dt.bfloat16
AF = mybir.ActivationFunctionType
ALU = mybir.AluOpType
AX = mybir.AxisListType

L = 2
NT = 2048
NS = 16
NCOL = NT + NS
TTS = [(0, 512), (512, 512), (1024, 512), (1536, 512), (2048, 16)]
PAD = 32
EPS = 1e-6
PL = 120
NPV = 2 * PL + 104
NRING = 2


class Buf:
    __slots__ = ("w", "r", "reg", "excl", "pending")

    def __init__(self, reg=None, excl=False):
        self.w = None
        self.r = {}
        self.reg = reg
        self.pending = False
        self.excl = excl


class Region:
    def __init__(self):
        self.prev = {}
        self.cur = {}

    def switch(self):
        for s, v in self.cur.items():
            if self.prev.get(s, 0) < v:
                self.prev[s] = v
        self.cur = {}


class Prog:
    ENG = ("pe", "act", "dve", "pool", "sp")

    def __init__(self, nc, stack):
        self.nc = nc
        self.stack = stack
        self.eng = {"pe": nc.tensor, "act": nc.scalar, "dve": nc.vector, "pool": nc.gpsimd, "sp": nc.sync}
        self.ops = {e: [] for e in self.ENG}
        self.semh = {}
        self.cnt = {}
        for e in self.ENG:
            self.newsem(e)
        self.banks = []
        self.bi = 0
        self.enabled = True
        self.reserved = set()
        self.last_idx = 0
        self.cutat = 1e9

    def cut(self, k):
        if k >= self.cutat:
            self.enabled = False

    def newsem(self, name):
        self.semh[name] = self.stack.enter_context(self.nc.semaphore("s_" + name))
        self.cnt[name] = 0
        return name

    def op(self, eng, fn, reads=(), writes=(), sem=None):
        if not self.enabled:
            return None
        deps = {}

        def add(s, v):
            if deps.get(s, 0) < v:
                deps[s] = v

        own = eng if sem is None else sem
        for b in reads:
            if b.w is not None:
                add(*b.w)
            if b.excl:
                for s, v in b.r.items():
                    if s != own:
                        add(s, v)
        for b in writes:
            if b.w is not None:
                add(*b.w)
            for s, v in b.r.items():
                add(s, v)
        for b in list(reads) + list(writes):
            if b.reg is not None:
                for s, v in b.reg.prev.items():
                    add(s, v)
        if sem is None:
            sname, inc = eng, 1
        else:
            sname, inc = sem, 16
        self.cnt[sname] += inc
        h = (sname, self.cnt[sname])
        self.ops[eng].append((deps, fn, h, inc))
        for b in reads:
            if b.r.get(sname, 0) < h[1]:
                b.r[sname] = h[1]
        for b in writes:
            b.w = h
            b.r = {}
            b.pending = True
        for b in reads:
            b.pending = False
        for b in list(reads) + list(writes):
            if b.reg is not None and b.reg.cur.get(sname, 0) < h[1]:
                b.reg.cur[sname] = h[1]
        return h

    def emit(self, ename, e):
        waited = {}
        for deps, fn, h, inc in self.ops[ename]:
            for s, v in deps.items():
                if s == "pe" and ename == "pe":
                    continue
                if waited.get(s, 0) >= v:
                    continue
                e.wait_ge(self.semh[s], v)
                waited[s] = v
            ins = fn(e)
            ins.then_inc(self.semh[h[0]], inc)

    def final_waits(self, e):
        for s, v in self.cnt.items():
            if v > 0:
                e.wait_ge(self.semh[s], v)

    def bank(self):
        i = self.bi
        n = 0
        while (i in self.reserved or self.banks[i][1].pending) and n < 8:
            i = (i + 1) % 8
            n += 1
        assert n < 8, "no free PSUM bank"
        self.bi = (i + 1) % 8
        self.last_idx = i
        return self.banks[i]


def build_nc():
    nc = bass.Bass("TRN2", target_bir_lowering=False)
    di = lambda n, s: nc.dram_tensor(n, s, F32, kind="ExternalInput").ap()
    do = lambda n, s: nc.dram_tensor(n, s, F32, kind="ExternalOutput").ap()
    xp = di("xp", [NT, 1024]); xs = di("xs", [NS, 1024]); mem = di("mem", [256, 1024])
    ck = di("ck", [L, NS, 256, 1024]); cv = di("cv", [L, NS, 256, 1024])
    sglu = di("sglu", [L, NS * 30, 256]); ssh = di("ssh", [L, NS * 2, 256]); spl = di("spl", [L, NS * 15, 256])
    w_in = di("w_in", [L, 1024, 2048]); w_out = di("w_out", [L, 1024, 1024])
    w_xq = di("w_xq", [L, 1024, 1024]); w_xk = di("w_xk", [L, 1024, 1024]); w_xv = di("w_xv", [L, 1024, 1024])
    w_xo = di("w_xo", [L, 1024, 1024]); w_ff1 = di("w_ff1", [L, 1024, 4096]); w_ff2 = di("w_ff2", [L, 4096, 1024])
    pvd = di("pv", [128, NPV]); bcd = di("bc", [L, 128, 512]); wstd = di("wst", [L, 128, 512])
    maskd = di("mask", [128, 128]); idfd = di("idf", [128, 128]); pwbd = di("pwb", [L, 128, 256])
    bsrd = di("bsrow", [L, 1, 512])
    o_yp = do("o_yp", [NT, 1024]); o_ys = do("o_ys", [NS, 1024])
    o_mk = do("o_mk", [L, 256, 1024]); o_mv = do("o_mv", [L, 256, 1024])
    o_glp = do("o_glp", [L, 30, 256]); o_gls = do("o_gls", [L, NS * 30, 256])
    o_shp = do("o_shp", [L, 2, 256]); o_shs = do("o_shs", [L, NS * 2, 256])
    o_plp = do("o_plp", [L, 15, 256]); o_pls = do("o_pls", [L, NS * 15, 256])
    o_gv = do("o_gv", [L, NS, 256])

    with contextlib.ExitStack() as st:
        P = Prog(nc, st)
        sb = lambda n, s, d: st.enter_context(nc.sbuf_tensor(n, s, d))
        XT = sb("XT", [128, 8, NCOL], F32)
        HT = sb("HT", [128, 8, NCOL], BF16)
        GT = sb("GT", [128, 8, NCOL], BF16)
        RING = sb("RING", [128, NRING, 8, 512], BF16)
        RS = sb("RS", [128, 2, 512], F32)
        TMP = sb("TMP", [128, 2, 512], F32)
        IDF = sb("IDF", [128, 128], F32)
        IDB = sb("IDB", [128, 128], BF16)
        ONESB = sb("ONESB", [128, 128], BF16)
        PV = sb("PV", [128, NPV], F32)
        EPST = sb("EPST", [128, 1], F32)
        STAT = sb("STAT", [128, 64], F32)
        SHW = 48 * 1024 // 2
        SH = sb("SH", [128, SHW], BF16)
        for i in range(8):
            P.banks.append((st.enter_context(nc.psum_tensor("bank%d" % i, [128, 512], F32)), Buf(excl=True)))

        reg = Region()

        class Carve:
            def __init__(self):
                self.off = 0

            def take(self, shape, dt):
                n = int(np.prod(shape))
                nb = n * (2 if dt == BF16 else 4)
                nb = (nb + 31) // 32 * 32
                ap = SH[:, self.off // 2:(self.off + nb) // 2]
                if dt == F32:
                    ap = ap.bitcast(F32)[:, 0:n]
                else:
                    ap = ap[:, 0:n]
                self.off += nb
                assert self.off <= SHW * 2, self.off
                if len(shape) == 2:
                    ap = ap.rearrange("p (a b) -> p a b", a=shape[0])
                elif len(shape) == 3:
                    ap = ap.rearrange("p (a b c) -> p a b c", a=shape[0], b=shape[1])
                return ap

        bXT = [[Buf() for _ in range(8)] for _ in TTS]
        bHT = [[Buf() for _ in range(8)] for _ in TTS]
        bGT = [[Buf() for _ in range(8)] for _ in TTS]
        bRING = [Buf() for _ in range(NRING)]
        bRS = [Buf(), Buf()]
        bTMP = [Buf(), Buf()]
        bCONST = Buf()
        rot = {"rs": 0, "tmp": 0, "ring": 0, "stat": 0}
        ringsem = [P.newsem("ring%d" % i) for i in range(NRING)]
        P.newsem("prm"); P.newsem("outs"); P.newsem("ldm"); P.newsem("pq"); P.newsem("pq0"); P.newsem("gvs")
        toutsem = [P.newsem("tout0"), P.newsem("tout1")]
        tmpsem = [P.newsem("tmpo0"), P.newsem("tmpo1")]

        def nxt(k, n=2):
            i = rot[k]
            rot[k] = (i + 1) % n
            return i

        def stat_cols(n):
            g = (n + 7) // 8
            i = rot["stat"]
            if i + g > 8:
                i = 0
            rot["stat"] = (i + g) % 8
            return STAT[:, i * 8:i * 8 + n], [bSTAT[i + q] for q in range(g)]

        bSTAT = [Buf() for _ in range(8)]

        P.op("sp", lambda e: e.dma_start(out=PV[:], in_=pvd[:, :]), writes=[bCONST], sem="prm")
        P.op("sp", lambda e: e.dma_start(out=IDF[:], in_=idfd[:, :]), writes=[bCONST], sem="prm")
        P.op("pool", lambda e: e.dma_start(out=IDB[:], in_=idfd[:, :]), writes=[bCONST], sem="pq0")
        P.op("dve", lambda e: e.memset(ONESB[:], 1.0), writes=[bCONST])
        P.op("dve", lambda e: e.memset(EPST[:], EPS), writes=[bCONST])

        P.cut(0.2)
        def load_block(src_ap):
            s = nxt("ring", NRING)
            assert not bRING[s].pending, "ring slot reloaded before its previous content was consumed"
            P.op("pool", lambda e, s=s: e.dma_start(out=RING[:, s], in_=src_ap.rearrange("(k p) n -> p k n", p=128)),
                 writes=[bRING[s]], sem=ringsem[s])
            return s

        def proj_block(s, in_t, b_in, evac, tts=range(5), otiles=range(4), nk=8):
            for tt in tts:
                c0, n = TTS[tt]
                if n <= 64:
                    shared = P.bank()
                    bks = [(shared[0][:, oi * n:(oi + 1) * n], shared[1]) for oi, _ in enumerate(otiles)]
                    wr = [shared[1]]
                else:
                    bks = [P.bank() for _ in otiles]
                    wr = [b[1] for b in bks]

                def fn(e, bks=bks, c0=c0, n=n):
                    ins = None
                    for oi, o in enumerate(otiles):
                        for k in range(nk):
                            ins = e.matmul(bks[oi][0][:, 0:n], lhsT=RING[:, s, k, o * 128:(o + 1) * 128],
                                           rhs=in_t[:, k, c0:c0 + n], start=(k == 0), stop=(k == nk - 1))
                    return ins
                P.op("pe", fn, reads=[bRING[s]] + b_in[tt][0:nk], writes=wr)
                for oi, o in enumerate(otiles):
                    evac(o, tt, bks[oi][0], bks[oi][1])

        def rstd_from_bank(bk, n, scale, np_=128):
            i = nxt("rs")
            P.op("act", lambda e: e.activation(out=RS[0:np_, i, 0:n], in_=bk[0][0:np_, 0:n], func=AF.Ln,
                                               bias=EPST[0:np_, :], scale=scale),
                 reads=[bk[1], bCONST], writes=[bRS[i]])
            P.op("act", lambda e: e.activation(out=RS[0:np_, i, 0:n], in_=RS[0:np_, i, 0:n], func=AF.Exp, scale=-0.5),
                 reads=[bRS[i]], writes=[bRS[i]])
            return i

        def rmsnorm(src, bsrc, k0, nk, gcol, dst, bdst, scr, bscr, tt, inv_n):
            c0, n = TTS[tt]
            P.op("act", lambda e: e.activation(out=scr[:, k0:k0 + nk, c0:c0 + n], in_=src[:, k0:k0 + nk, c0:c0 + n],
                                               func=AF.Square),
                 reads=bsrc[tt][k0:k0 + nk], writes=bscr[tt][k0:k0 + nk])
            bk = P.bank()

            def fn(e):
                ins = None
                for k in range(nk):
                    ins = e.matmul(bk[0][:, 0:n], lhsT=ONESB[:, :], rhs=scr[:, k0 + k, c0:c0 + n],
                                   start=(k == 0), stop=(k == nk - 1))
                return ins
            P.op("pe", fn, reads=bscr[tt][k0:k0 + nk] + [bCONST], writes=[bk[1]])
            i = rstd_from_bank(bk, n, inv_n)
            for k in range(nk):
                eng = "dve"
                P.op(eng, lambda e, k=k: e.scalar_tensor_tensor(
                    out=dst[:, k0 + k, c0:c0 + n], in0=src[:, k0 + k, c0:c0 + n], scalar=PV[:, gcol + k:gcol + k + 1],
                    in1=RS[:, i, 0:n], op0=ALU.mult, op1=ALU.mult),
                    reads=[bsrc[tt][k0 + k], bRS[i], bCONST], writes=[bdst[tt][k0 + k]])

        def evac_add_x(obase):
            def f(o, tt, bk, bb):
                c0, n = TTS[tt]
                P.op("dve", lambda e: e.tensor_tensor(out=XT[:, obase + o, c0:c0 + n], in0=bk[:, 0:n],
                                                      in1=XT[:, obase + o, c0:c0 + n], op=ALU.add),
                     reads=[bb], writes=[bXT[tt][obase + o]])
            return f

        def transpose_out(src_fn, np_in, nfree, dt):
            pass

        cv0 = Carve()
        XST = cv0.take([2, 1024], F32)
        bXST = [Buf(reg), Buf(reg)]
        xsem = [P.newsem("xin0"), P.newsem("xin1")]
        def load_x_tile(c):
            np_ = 128 if c < 16 else NS
            sl = c % 2
            src = xp[c * 128:(c + 1) * 128, :] if c < 16 else xs[:, :]
            P.op("sp", lambda e, sl=sl, np_=np_, src=src: e.dma_start(out=XST[0:np_, sl, :], in_=src),
                 writes=[bXST[sl]], sem=xsem[sl])
            tt = c // 4 if c < 16 else 4
            col = c * 128
            for half in range(2):
                bk = P.bank()

                def fn(e, bk=bk, sl=sl, np_=np_, half=half):
                    ins = None
                    for q in range(4):
                        f = half * 4 + q
                        ins = e.transpose(out=bk[0][:, q * 128:q * 128 + np_], in_=XST[0:np_, sl, f * 128:(f + 1) * 128],
                                          identity=IDF[0:np_, 0:np_])
                    return ins
                P.op("pe", fn, reads=[bXST[sl], bCONST], writes=[bk[1]])
                eng = "act" if half == 0 else "dve"
                if eng == "act":
                    P.op("act", lambda e, bk=bk, half=half, col=col, np_=np_: e.activation(
                        out=XT[:, half * 4:half * 4 + 4, col:col + np_],
                        in_=bk[0][:, :].rearrange("p (q t) -> p q t", q=4)[:, :, 0:np_], func=AF.Copy),
                        reads=[bk[1]], writes=bXT[tt][half * 4:half * 4 + 4])
                else:
                    P.op("dve", lambda e, bk=bk, half=half, col=col, np_=np_: e.tensor_copy(
                        out=XT[:, half * 4:half * 4 + 4, col:col + np_],
                        in_=bk[0][:, :].rearrange("p (q t) -> p q t", q=4)[:, :, 0:np_]),
                        reads=[bk[1]], writes=bXT[tt][half * 4:half * 4 + 4])

        for c in range(17):
            load_x_tile(c)
            P.cut(0.5 + c * 0.01)
        P.cut(1)

        def layer(l):
            pb = l * PL
            reg.switch()
            cm = Carve()
            MA = cm.take([2, PAD + NCOL], BF16); MB = cm.take([2, PAD + NCOL], BF16); MC = cm.take([2, PAD + NCOL], BF16)
            DG = cm.take([8, 128], BF16)
            BCT = cm.take([512], F32)
            WSTB = cm.take([4, 128], BF16); WMT = cm.take([4, 128], BF16); MASKB = cm.take([128], BF16)
            PWB = cm.take([2, 128], BF16)
            BSROW = cm.take([4, 128], BF16)
            ONEROW = cm.take([128], BF16)
            VT = cm.take([2, 256], F32)
            VN = cm.take([4, 256], BF16)
            VNS = cm.take([256], F32)
            ST = cm.take([2, NS * 30], F32); STP = cm.take([2, NS * 15], F32); STS = cm.take([2, NS * 2], F32)
            SMP = cm.take([8, NS], F32)
            TOUT = cm.take([2, 256], F32)
            R = lambda: Buf(reg)
            bMA = [[R(), R()] for _ in TTS]; bMB = [[R(), R()] for _ in TTS]; bMC = [[R(), R()] for _ in TTS]
            bDG = [R() for _ in range(8)]
            bPRM = R(); bWMT = R(); bVT = [R() for _ in range(2)]; bVN = [R() for _ in range(4)]; bVNS = R()
            bST = R(); bSTP = R(); bSTS = R(); bSMP = [R() for _ in range(8)]; bTOUT = [R(), R()]
            bPADS = R()
            dgi = [0]
            touti = [0]

            P.op("sp", lambda e: e.dma_start(out=BCT[:], in_=bcd[l]), writes=[bPRM], sem="ldm")
            P.op("pool", lambda e: e.dma_start(out=WSTB[:], in_=wstd[l].rearrange("p (h t) -> p h t", h=4)), writes=[bPRM], sem="pq")
            P.op("pool", lambda e: e.dma_start(out=MASKB[:], in_=maskd[:, :]), writes=[bPRM], sem="pq")
            P.op("pool", lambda e: e.dma_start(out=PWB[:], in_=pwbd[l].rearrange("p (j c) -> p j c", j=2)), writes=[bPRM], sem="pq")
            P.op("pool", lambda e: e.dma_start(out=BSROW[0:1], in_=bsrd[l].rearrange("o (h t) -> o h t", h=4)), writes=[bPRM], sem="pq")
            P.op("dve", lambda e: e.memset(ONEROW[0:1, :], 1.0), writes=[bPRM])
            for h in range(4):
                P.op("dve", lambda e, h=h: e.tensor_tensor(out=WMT[:, h, :], in0=WSTB[:, h, :], in1=MASKB[:, :], op=ALU.mult),
                     reads=[bPRM], writes=[bWMT])
            for M_ in (MA, MB, MC):
                P.op("dve", lambda e, M_=M_: e.memset(M_[:, :, 0:PAD], 0.0), writes=[bPADS])
            ldsem = "ldm"
            for (srcd, rows_per, dstT, bdst, ncol) in ((sglu, 120, ST, bST, NS * 30), (spl, 120, STP, bSTP, NS * 15),
                                                      (ssh, 32, STS, bSTS, NS * 2)):
                ntile = ncol // rows_per
                for i in range(ntile):
                    ti = nxt("tmp")
                    P.op("sp", lambda e, ti=ti, i=i, srcd=srcd, rows_per=rows_per: e.dma_start(
                        out=TMP[0:rows_per, ti, 0:256], in_=srcd[l, i * rows_per:(i + 1) * rows_per, :]),
                        writes=[bTMP[ti]], sem=tmpsem[ti])
                    bk = P.bank()

                    def fn(e, bk=bk, ti=ti, rows_per=rows_per):
                        ins = None
                        for j in range(2):
                            ins = e.transpose(out=bk[0][:, j * 128:j * 128 + rows_per], in_=TMP[0:rows_per, ti, j * 128:(j + 1) * 128],
                                              identity=IDF[0:rows_per, 0:rows_per])
                        return ins
                    P.op("pe", fn, reads=[bTMP[ti], bCONST], writes=[bk[1]])
                    P.op("dve", lambda e, bk=bk, i=i, rows_per=rows_per, dstT=dstT: e.tensor_copy(
                        out=dstT[:, :, i * rows_per:(i + 1) * rows_per],
                        in_=bk[0][:, 0:256].rearrange("p (j r) -> p j r", j=2)[:, :, 0:rows_per]),
                        reads=[bk[1]], writes=[bdst])

            rmsnorm(XT, bXT, 0, 8, pb + 0, HT, bHT, HT, bHT, 0, 1.0 / 1024)

            P.cut(15 * l + 2)

            def new_diag(col):
                i = dgi[0]
                dgi[0] = (i + 1) % 8
                P.op("dve", lambda e: e.tensor_scalar(out=DG[:, i, :], in0=IDB[:, :], scalar1=PV[:, col:col + 1], scalar2=None,
                                                      op0=ALU.mult),
                     reads=[bCONST], writes=[bDG[i]])
                return i

            def out_tokmajor(src_fn, ncols, dst_ap, bsrc):
                oi = touti[0]
                touti[0] = (oi + 1) % 2
                bk = P.bank()
                bkb = bk[0][:, :].bitcast(BF16)

                def fn(e):
                    ins = None
                    for j in range(2):
                        ins = e.transpose(out=bkb[0:ncols, j * 128:(j + 1) * 128], in_=src_fn(j), identity=IDB[:, :])
                    return ins
                P.op("pe", fn, reads=bsrc + [bCONST], writes=[bk[1]])
                P.op("dve", lambda e: e.tensor_copy(out=TOUT[0:ncols, oi, :], in_=bkb[0:ncols, 0:256]),
                     reads=[bk[1]], writes=[bTOUT[oi]])
                P.op("sp", lambda e: e.dma_start(out=dst_ap, in_=TOUT[0:ncols, oi, :]), reads=[bTOUT[oi]], sem=toutsem[oi])

            s0 = load_block(w_in[l, :, 0:512])
            s1 = load_block(w_in[l, :, 512:1024])

            def evac_u(o, tt, bk, bb):
                c0, n = TTS[tt]
                P.op("act", lambda e: e.activation(out=MA[:, o, PAD + c0:PAD + c0 + n], in_=bk[:, 0:n], func=AF.Copy),
                     reads=[bb], writes=[bMA[tt][o]])
            def gmlp_tile(tt, mid):
                c0, n = TTS[tt]
                proj_block(s0, HT, bHT, evac_u, tts=[tt], otiles=range(2))
                nch = 4 if tt < 4 else 1
                np_ = 128 if tt < 4 else NS
                vb = [P.bank() for _ in range((nch + 1) // 2)]

                def fnv(e, vb=vb, c0=c0, nch=nch, np_=np_):
                    ins = None
                    for c in range(nch):
                        for k in range(8):
                            ins = e.matmul(vb[c // 2][0][0:np_, (c % 2) * 256:(c % 2) * 256 + 256],
                                           lhsT=HT[:, k, c0 + c * 128:c0 + c * 128 + np_], rhs=RING[:, s0, k, 256:512],
                                           start=(k == 0), stop=(k == 7))
                    return ins
                P.op("pe", fnv, reads=[bRING[s0]] + bHT[tt], writes=[b[1] for b in vb])
                mv, bmv = stat_cols(8 + 24)
                for c in range(nch):
                    vsrc = vb[c // 2][0][0:np_, (c % 2) * 256:(c % 2) * 256 + 256]
                    P.op("dve", lambda e, vsrc=vsrc, c=c: e.bn_stats(out=mv[0:np_, 8 + 6 * c:8 + 6 * c + 6], in_=vsrc),
                         reads=[vb[c // 2][1]], writes=bmv)
                    P.op("dve", lambda e, c=c: e.bn_aggr(out=mv[0:np_, 2 * c:2 * c + 2], in_=mv[0:np_, 8 + 6 * c:8 + 6 * c + 6]),
                         reads=bmv, writes=bmv)
                mv3 = mv[0:np_, 0:2 * nch].rearrange("p (c t) -> p c t", t=2)
                P.op("act", lambda e: e.activation(out=mv3[:, :, 1:2], in_=mv3[:, :, 1:2], func=AF.Ln, bias=EPST[0:np_, :]),
                     reads=bmv + [bCONST], writes=bmv)
                P.op("act", lambda e: e.activation(out=mv3[:, :, 1:2], in_=mv3[:, :, 1:2], func=AF.Exp, scale=-0.5),
                     reads=bmv, writes=bmv)
                scr3 = mv[0:np_, 8:8 + 6 * nch].rearrange("p (c s) -> p c s", s=6)
                P.op("dve", lambda e: e.scalar_tensor_tensor(out=scr3[:, :, 0:1], in0=mv3[:, :, 0:1], scalar=-1.0, in1=mv3[:, :, 1:2],
                                                             op0=ALU.mult, op1=ALU.mult), reads=bmv, writes=bmv)
                for c in range(nch):
                    vsrc = vb[c // 2][0][0:np_, (c % 2) * 256:(c % 2) * 256 + 256]
                    P.op("act", lambda e, vsrc=vsrc, c=c: e.activation(
                        out=VT[0:np_, c % 2, :], in_=vsrc, func=AF.Identity, scale=mv[0:np_, 2 * c + 1:2 * c + 2],
                        bias=mv[0:np_, 8 + 6 * c:8 + 6 * c + 1]), reads=[vb[c // 2][1]] + bmv, writes=[bVT[c % 2]])
                    P.op("dve", lambda e, c=c: e.tensor_tensor(out=VT[0:np_, c % 2, :], in0=VT[0:np_, c % 2, :], in1=BCT[0:np_, 0:256], op=ALU.mult),
                         reads=[bVT[c % 2], bPRM], writes=[bVT[c % 2]])
                    if tt < 4:
                        P.op("dve", lambda e, c=c: e.tensor_tensor(out=VN[:, c, :], in0=VT[:, c % 2, :], in1=BCT[:, 256:512], op=ALU.add),
                             reads=[bVT[c % 2], bPRM], writes=[bVN[c]])
                    else:
                        P.op("dve", lambda e: e.tensor_tensor(out=VNS[0:NS, :], in0=VT[0:NS, 0, :], in1=BCT[0:NS, 256:512], op=ALU.add),
                             reads=[bVT[0], bPRM], writes=[bVNS])
                mid()
                if tt < 4:
                    zb = [P.bank() for _ in range(4)]

                    def fnz(e, zb=zb):
                        ins = None
                        for c in range(4):
                            for pr in range(2):
                                for ee in range(2):
                                    h = 2 * pr + ee
                                    outp = zb[pr * 2 + ee][0][:, c * 128:(c + 1) * 128]
                                    e.matmul(outp, lhsT=ONEROW[0:1, :], rhs=BSROW[0:1, h, :], start=True, stop=False)
                                    ins = e.matmul(outp, lhsT=VN[:, c, pr * 128:(pr + 1) * 128], rhs=WMT[:, h, :], start=False, stop=True)
                        return ins
                    P.op("pe", fnz, reads=bVN + [bWMT, bPRM], writes=[b[1] for b in zb])
                    for pr in range(2):
                        for ee in range(2):
                            zz = zb[pr * 2 + ee]
                            P.op("dve", lambda e, zz=zz, pr=pr, ee=ee, c0=c0: e.tensor_tensor(
                                out=GT[ee * 64:(ee + 1) * 64, pr, c0:c0 + 512], in0=zz[0][ee * 64:(ee + 1) * 64, :],
                                in1=MA[ee * 64:(ee + 1) * 64, pr, PAD + c0:PAD + c0 + 512], op=ALU.mult),
                                reads=[zz[1], bMA[tt][pr]], writes=[bGT[tt][pr]])
                else:
                    P.op("sp", lambda e: e.dma_start(out=o_gv[l], in_=VNS[0:NS, :]), reads=[bVNS], sem="gvs")
                    bk = P.bank()

                    def fnt(e, bk=bk):
                        ins = None
                        for j in range(2):
                            ins = e.transpose(out=bk[0][:, j * NS:(j + 1) * NS], in_=VNS[0:NS, j * 128:(j + 1) * 128], identity=IDF[0:NS, 0:NS])
                        return ins
                    P.op("pe", fnt, reads=[bVNS, bCONST], writes=[bk[1]])
                    for j in range(2):
                        P.op("dve", lambda e, bk=bk, j=j: e.tensor_scalar(
                            out=SMP[:, j, :], in0=bk[0][:, j * NS:(j + 1) * NS], scalar1=PV[:, pb + 48 + j:pb + 49 + j],
                            scalar2=PV[:, pb + 50 + j:pb + 51 + j], op0=ALU.mult, op1=ALU.add),
                            reads=[bk[1], bCONST], writes=[bSMP[j]])
                        P.op("dve", lambda e, j=j: e.tensor_tensor(out=GT[:, j, NT:NCOL], in0=SMP[:, j, :], in1=MA[:, j, PAD + NT:PAD + NCOL], op=ALU.mult),
                             reads=[bSMP[j], bMA[4][j]], writes=[bGT[4][j]])

            def evac_glu(tt, bks):
                c0, n = TTS[tt]
                for j in range(2):
                    P.op("act", lambda e, j=j: e.activation(out=MB[:, j, PAD + c0:PAD + c0 + n], in_=bks[j][0][:, 0:n], func=AF.Copy),
                         reads=[bks[j][1]], writes=[bMB[tt][j]])
                    P.op("act", lambda e, j=j: e.activation(out=MC[:, j, PAD + c0:PAD + c0 + n], in_=bks[2 + j][0][:, 0:n], func=AF.Copy),
                         reads=[bks[2 + j][1]], writes=[bMC[tt][j]])
            def glu_tile(tt):
                c0, n = TTS[tt]
                bks = [P.bank() for _ in range(4)]

                def fn(e):
                    ins = None
                    for o in range(4):
                        for k in range(8):
                            ins = e.matmul(bks[o][0][:, 0:n], lhsT=RING[:, s1, k, o * 128:(o + 1) * 128], rhs=HT[:, k, c0:c0 + n],
                                           start=(k == 0), stop=(k == 7))
                    return ins
                P.op("pe", fn, reads=[bRING[s1]] + bHT[tt], writes=[b[1] for b in bks])
                evac_glu(tt, bks)
            def gm_mid(tt):
                glu_tile(tt)
                if tt + 1 < 5:
                    rmsnorm(XT, bXT, 0, 8, pb + 0, HT, bHT, HT, bHT, tt + 1, 1.0 / 1024)
            for tt in range(5):
                gmlp_tile(tt, lambda tt=tt: gm_mid(tt))

            def glu_finish(tt):
                c0, n = TTS[tt]
                for j in range(2):
                    P.op("act", lambda e, j=j: e.activation(out=MC[:, j, PAD + c0:PAD + c0 + n], in_=MC[:, j, PAD + c0:PAD + c0 + n], func=AF.Sigmoid),
                         reads=[bMC[tt][j]], writes=[bMC[tt][j]])
                    P.op("dve", lambda e, j=j: e.tensor_tensor(out=MB[:, j, PAD + c0:PAD + c0 + n], in0=MB[:, j, PAD + c0:PAD + c0 + n],
                                                               in1=MC[:, j, PAD + c0:PAD + c0 + n], op=ALU.mult),
                         reads=[bMC[tt][j]], writes=[bMB[tt][j]])
            for tt in range(5):
                glu_finish(tt)

            P.cut(15 * l + 3)
            s2 = load_block(w_in[l, :, 1024:1536])

            out_tokmajor(lambda j: MB[:, j, PAD + NT - 30:PAD + NT], 30, o_glp[l], [bMB[3][0], bMB[3][1]])
            P.op("sp", lambda e: e.dma_start(out=o_gls[l].rearrange("(b k) c -> b k c", k=30)[:, 0:29, :],
                                             in_=sglu[l].rearrange("(b k) c -> b k c", k=30)[:, 1:30, :]), sem="outs")
            out_tokmajor(lambda j: MB[:, j, PAD + NT:PAD + NCOL], NS, o_gls[l].rearrange("(b k) c -> b k c", k=30)[:, 29, :],
                         [bMB[4][0], bMB[4][1]])

            P.cut(15 * l + 4)

            def conv_prompt(src, bsrc, ntap, wcol, tapoff, evac):
                for j in range(2):
                    bks = [P.bank() for _ in range(4)]
                    for k in range(ntap):
                        di_ = new_diag(wcol(j, k))

                        def fn(e, di_=di_, k=k, j=j, bks=bks):
                            ins = None
                            for tt in range(4):
                                c0 = TTS[tt][0]
                                a = PAD + c0 + tapoff(k)
                                ins = e.matmul(bks[tt][0][:, :], lhsT=DG[:, di_, :], rhs=src[:, j, a:a + 512],
                                               start=(k == 0), stop=(k == ntap - 1))
                            return ins
                        rd = [bDG[di_], bPADS] + [bsrc[tt][j] for tt in range(4)]
                        P.op("pe", fn, reads=rd, writes=[b[1] for b in bks])
                    for tt in range(4):
                        evac(j, tt, bks[tt])

            def evac_cb(j, tt, bk):
                c0, n = TTS[tt]
                P.op("act", lambda e: e.activation(out=GT[:, 2 + j, c0:c0 + n], in_=bk[0][:, 0:n], func=AF.Identity,
                                                   bias=PV[:, pb + 40 + j:pb + 41 + j]),
                     reads=[bk[1], bCONST], writes=[bGT[tt][2 + j]])
                P.op("act", lambda e: e.activation(out=GT[:, 6 + j, c0:c0 + n], in_=bk[0][:, 0:n], func=AF.Square,
                                                   bias=PV[:, pb + 40 + j:pb + 41 + j]),
                     reads=[bk[1], bCONST], writes=[bGT[tt][6 + j]])

            s3 = load_block(w_in[l, :, 1536:2048])

            def evac_b2(o, tt, bk, bb):
                c0, n = TTS[tt]
                if o < 2:
                    P.op("act", lambda e: e.activation(out=MA[:, o, PAD + c0:PAD + c0 + n], in_=bk[:, 0:n], func=AF.Copy),
                         reads=[bb], writes=[bMA[tt][o]])
                else:
                    P.op("act", lambda e: e.activation(out=MC[:, o - 2, PAD + c0:PAD + c0 + n], in_=bk[:, 0:n], func=AF.Copy),
                         reads=[bb], writes=[bMC[tt][o - 2]])
            proj_block(s2, HT, bHT, evac_b2)

            conv_prompt(MB, bMB, 31, lambda j, k: pb + 52 + j * 31 + k, lambda k: k - 30, evac_cb)
            for j in range(2):
                ti = nxt("tmp")
                P.op("dve", lambda e, j=j, ti=ti: e.tensor_tensor(
                    out=TMP[:, ti, 0:NS * 30].rearrange("p (b k) -> p b k", k=30),
                    in0=ST[:, j, :].rearrange("p (b k) -> p b k", k=30),
                    in1=PV[:, pb + 52 + j * 31:pb + 52 + j * 31 + 30].unsqueeze(1).broadcast_to([128, NS, 30]), op=ALU.mult),
                    reads=[bST, bCONST], writes=[bTMP[ti]])
                P.op("dve", lambda e, j=j, ti=ti: e.tensor_reduce(out=SMP[:, 2 + j, :], in_=TMP[:, ti, 0:NS * 30].rearrange("p (b k) -> p b k", k=30),
                                                                  axis=AX.X, op=ALU.add),
                     reads=[bTMP[ti]], writes=[bSMP[2 + j]])
                P.op("dve", lambda e, j=j: e.scalar_tensor_tensor(out=SMP[:, 2 + j, :], in0=MB[:, j, PAD + NT:PAD + NCOL],
                                                                  scalar=PV[:, pb + 52 + j * 31 + 30:pb + 52 + j * 31 + 31],
                                                                  in1=SMP[:, 2 + j, :], op0=ALU.mult, op1=ALU.add),
                     reads=[bMB[4][j], bSMP[2 + j], bCONST], writes=[bSMP[2 + j]])
                P.op("act", lambda e, j=j: e.activation(out=GT[:, 2 + j, NT:NCOL], in_=SMP[:, 2 + j, :], func=AF.Identity,
                                                        bias=PV[:, pb + 40 + j:pb + 41 + j]),
                     reads=[bSMP[2 + j], bCONST], writes=[bGT[4][2 + j]])
                P.op("act", lambda e, j=j: e.activation(out=GT[:, 6 + j, NT:NCOL], in_=SMP[:, 2 + j, :], func=AF.Square,
                                                        bias=PV[:, pb + 40 + j:pb + 41 + j]),
                     reads=[bSMP[2 + j], bCONST], writes=[bGT[4][6 + j]])

            P.cut(15 * l + 5)
            def evac_b3(o, tt, bk, bb):
                c0, n = TTS[tt]
                if o < 2:
                    P.op("dve", lambda e: e.tensor_tensor(out=MC[:, o, PAD + c0:PAD + c0 + n], in0=bk[:, 0:n],
                                                          in1=MC[:, o, PAD + c0:PAD + c0 + n], op=ALU.mult),
                         reads=[bb], writes=[bMC[tt][o]])
                else:
                    P.op("act", lambda e: e.activation(out=MB[:, o - 2, PAD + c0:PAD + c0 + n], in_=bk[:, 0:n], func=AF.Copy),
                         reads=[bb], writes=[bMB[tt][o - 2]])

            def conf_ln_b(tt, j, c0, n):
                P.op("act", lambda e: e.activation(out=GT[:, 6 + j, c0:c0 + n], in_=GT[:, 2 + j, c0:c0 + n], func=AF.Sigmoid,
                                                                   scale=PV[:, pb + 42 + j:pb + 43 + j], bias=PV[:, pb + 44 + j:pb + 45 + j]),
                     reads=[bGT[tt][2 + j], bCONST], writes=[bGT[tt][6 + j]])
                P.op("dve", lambda e: e.tensor_scalar(out=GT[:, 2 + j, c0:c0 + n], in0=GT[:, 2 + j, c0:c0 + n],
                                                                      scalar1=PV[:, pb + 42 + j:pb + 43 + j], scalar2=PV[:, pb + 44 + j:pb + 45 + j],
                                                                      op0=ALU.mult, op1=ALU.add),
                     reads=[bGT[tt][6 + j], bCONST], writes=[bGT[tt][2 + j]])
                P.op("dve", lambda e: e.tensor_tensor(out=GT[:, 2 + j, c0:c0 + n], in0=GT[:, 2 + j, c0:c0 + n],
                                                                      in1=GT[:, 6 + j, c0:c0 + n], op=ALU.mult),
                     reads=[bGT[tt][2 + j], bGT[tt][6 + j]], writes=[bGT[tt][2 + j]])

            def conf_ln_tile(tt, part):
                c0, n = TTS[tt]
                if part == 1:
                    for j in range(2):
                        conf_ln_b(tt, j, c0, n)
                    return
                bm = P.bank(); bq = P.bank()

                def fn(e, bm=bm, bq=bq, c0=c0, n=n):
                    ins = None
                    for j in range(2):
                        e.matmul(bm[0][:, 0:n], lhsT=ONESB[:, :], rhs=GT[:, 2 + j, c0:c0 + n], start=(j == 0), stop=(j == 1))
                    for j in range(2):
                        ins = e.matmul(bq[0][:, 0:n], lhsT=ONESB[:, :], rhs=GT[:, 6 + j, c0:c0 + n], start=(j == 0), stop=(j == 1))
                    return ins
                P.op("pe", fn, reads=bGT[tt][2:4] + bGT[tt][6:8] + [bCONST], writes=[bm[1], bq[1]])
                t0 = nxt("tmp")
                P.op("dve", lambda e, t0=t0, bm=bm, n=n: e.tensor_scalar(out=TMP[:, t0, 0:n], in0=bm[0][:, 0:n], scalar1=1.0 / 256, scalar2=None,
                                                                      op0=ALU.mult), reads=[bm[1]], writes=[bTMP[t0]])
                t1 = nxt("tmp")
                P.op("dve", lambda e, t0=t0, t1=t1, n=n: e.tensor_tensor(out=TMP[:, t1, 0:n], in0=TMP[:, t0, 0:n], in1=TMP[:, t0, 0:n], op=ALU.mult),
                     reads=[bTMP[t0]], writes=[bTMP[t1]])
                P.op("dve", lambda e, t1=t1, bq=bq, n=n: e.scalar_tensor_tensor(out=TMP[:, t1, 0:n], in0=bq[0][:, 0:n], scalar=1.0 / 256,
                                                                               in1=TMP[:, t1, 0:n], op0=ALU.mult, op1=ALU.subtract),
                     reads=[bq[1], bTMP[t1]], writes=[bTMP[t1]])
                ri = nxt("rs")
                P.op("act", lambda e, t1=t1, ri=ri, n=n: e.activation(out=RS[:, ri, 0:n], in_=TMP[:, t1, 0:n], func=AF.Ln, bias=EPST[:, :]),
                     reads=[bTMP[t1], bCONST], writes=[bRS[ri]])
                P.op("act", lambda e, ri=ri, n=n: e.activation(out=RS[:, ri, 0:n], in_=RS[:, ri, 0:n], func=AF.Exp, scale=-0.5),
                     reads=[bRS[ri]], writes=[bRS[ri]])
                for j in range(2):
                    P.op("dve", lambda e, j=j, t0=t0, c0=c0, n=n: e.tensor_tensor(out=GT[:, 2 + j, c0:c0 + n], in0=GT[:, 2 + j, c0:c0 + n],
                                                                                 in1=TMP[:, t0, 0:n], op=ALU.subtract),
                         reads=[bTMP[t0]], writes=[bGT[tt][2 + j]])
                    P.op("dve", lambda e, j=j, ri=ri, c0=c0, n=n: e.tensor_tensor(out=GT[:, 2 + j, c0:c0 + n], in0=GT[:, 2 + j, c0:c0 + n],
                                                                                 in1=RS[:, ri, 0:n], op=ALU.mult),
                         reads=[bRS[ri]], writes=[bGT[tt][2 + j]])
            for tt in range(5):
                conf_ln_tile(tt, 0)
                proj_block(s3, HT, bHT, evac_b3, tts=[tt])

            P.cut(15 * l + 6)
            out_tokmajor(lambda j: MC[:, j, PAD + NT - 2:PAD + NT], 2, o_shp[l], [bMC[3][0], bMC[3][1]])
            P.op("sp", lambda e: e.dma_start(out=o_shs[l].rearrange("(b k) c -> b k c", k=2)[:, 0, :],
                                             in_=ssh[l].rearrange("(b k) c -> b k c", k=2)[:, 1, :]), sem="outs")
            out_tokmajor(lambda j: MC[:, j, PAD + NT:PAD + NCOL], NS, o_shs[l].rearrange("(b k) c -> b k c", k=2)[:, 1, :],
                         [bMC[4][0], bMC[4][1]])

            def evac_c(j, tt, bk):
                c0, n = TTS[tt]
                P.op("dve", lambda e: e.tensor_tensor(out=GT[:, 4 + j, c0:c0 + n], in0=bk[0][:, 0:n], in1=MA[:, j, PAD + c0:PAD + c0 + n], op=ALU.mult),
                     reads=[bk[1], bMA[tt][j]], writes=[bGT[tt][4 + j]])
            conv_prompt(MC, bMC, 3, lambda j, k: pb + 114 + j * 3 + k, lambda k: k - 2, evac_c)
            for j in range(2):
                sts = STS[:, j, :].rearrange("p (b k) -> p b k", k=2)
                P.op("dve", lambda e, j=j, sts=sts: e.tensor_scalar(out=SMP[:, 4 + j, :], in0=sts[:, :, 0], scalar1=PV[:, pb + 114 + j * 3:pb + 115 + j * 3],
                                                                    scalar2=None, op0=ALU.mult), reads=[bSTS, bCONST], writes=[bSMP[4 + j]])
                P.op("dve", lambda e, j=j, sts=sts: e.scalar_tensor_tensor(out=SMP[:, 4 + j, :], in0=sts[:, :, 1], scalar=PV[:, pb + 115 + j * 3:pb + 116 + j * 3],
                                                                           in1=SMP[:, 4 + j, :], op0=ALU.mult, op1=ALU.add),
                     reads=[bSTS, bSMP[4 + j], bCONST], writes=[bSMP[4 + j]])
                P.op("dve", lambda e, j=j: e.scalar_tensor_tensor(out=SMP[:, 4 + j, :], in0=MC[:, j, PAD + NT:PAD + NCOL], scalar=PV[:, pb + 116 + j * 3:pb + 117 + j * 3],
                                                                  in1=SMP[:, 4 + j, :], op0=ALU.mult, op1=ALU.add),
                     reads=[bMC[4][j], bSMP[4 + j], bCONST], writes=[bSMP[4 + j]])
                P.op("dve", lambda e, j=j: e.tensor_tensor(out=GT[:, 4 + j, NT:NCOL], in0=SMP[:, 4 + j, :], in1=MA[:, j, PAD + NT:PAD + NCOL], op=ALU.mult),
                     reads=[bSMP[4 + j], bMA[4][j]], writes=[bGT[4][4 + j]])

            P.cut(15 * l + 7)
            out_tokmajor(lambda j: MB[:, j, PAD + NT - 15:PAD + NT], 15, o_plp[l], [bMB[3][0], bMB[3][1]])
            P.op("sp", lambda e: e.dma_start(out=o_pls[l].rearrange("(b k) c -> b k c", k=15)[:, 0:14, :],
                                             in_=spl[l].rearrange("(b k) c -> b k c", k=15)[:, 1:15, :]), sem="outs")
            out_tokmajor(lambda j: MB[:, j, PAD + NT:PAD + NCOL], NS, o_pls[l].rearrange("(b k) c -> b k c", k=15)[:, 14, :],
                         [bMB[4][0], bMB[4][1]])
            DV = 2 * PL + 8
            RT = 2 * PL + 40

            def evac_pool(j, tt, bk):
                c0, n = TTS[tt]
                P.op("act", lambda e: e.activation(out=MC[:, j, PAD + c0:PAD + c0 + n], in_=bk[0][:, 0:n], func=AF.Copy),
                     reads=[bk[1]], writes=[bMC[tt][j]])
                if tt == 0:
                    P.op("dve", lambda e: e.tensor_tensor(out=SMP[:, 6 + j, 0:15], in0=bk[0][:, 0:15], in1=MB[:, j, PAD:PAD + 15], op=ALU.add),
                         reads=[bk[1], bMB[0][j]], writes=[bSMP[6 + j]])
                    P.op("dve", lambda e: e.tensor_tensor(out=SMP[:, 6 + j, 0:15], in0=SMP[:, 6 + j, 0:15], in1=PV[:, RT + j * 16:RT + j * 16 + 15], op=ALU.mult),
                         reads=[bSMP[6 + j], bCONST], writes=[bSMP[6 + j]])
                    P.op("dve", lambda e: e.tensor_tensor(out=MC[:, j, PAD:PAD + 15], in0=SMP[:, 6 + j, 0:15], in1=MB[:, j, PAD:PAD + 15], op=ALU.subtract),
                         reads=[bSMP[6 + j], bMB[0][j]], writes=[bMC[0][j]])
            conv_prompt(MB, bMB, 16, lambda j, k: DV + j * 16 + (15 - k), lambda k: k - 15, evac_pool)
            for j in range(2):
                ti = nxt("tmp")
                P.op("dve", lambda e, j=j, ti=ti: e.tensor_tensor(
                    out=TMP[:, ti, 0:NS * 15].rearrange("p (b k) -> p b k", k=15),
                    in0=STP[:, j, :].rearrange("p (b k) -> p b k", k=15),
                    in1=PV[:, RT + 32 + j * 16:RT + 32 + j * 16 + 15].unsqueeze(1).broadcast_to([128, NS, 15]), op=ALU.mult),
                    reads=[bSTP, bCONST], writes=[bTMP[ti]])
                P.op("dve", lambda e, j=j, ti=ti: e.tensor_reduce(out=SMP[:, 6 + j, :], in_=TMP[:, ti, 0:NS * 15].rearrange("p (b k) -> p b k", k=15),
                                                                  axis=AX.X, op=ALU.add), reads=[bTMP[ti]], writes=[bSMP[6 + j]])
                P.op("dve", lambda e, j=j: e.scalar_tensor_tensor(out=MC[:, j, PAD + NT:PAD + NCOL], in0=MB[:, j, PAD + NT:PAD + NCOL], scalar=PV[:, DV + j * 16:DV + j * 16 + 1],
                                                                  in1=SMP[:, 6 + j, :], op0=ALU.mult, op1=ALU.add),
                     reads=[bMB[4][j], bSMP[6 + j], bCONST], writes=[bMC[4][j]])
            for tt in range(5):
                conf_ln_tile(tt, 1)
            for tt in range(5):
                c0, n = TTS[tt]
                for j in range(2):
                    bk = P.bank()
                    P.op("pe", lambda e, bk=bk, j=j, c0=c0, n=n: e.matmul(bk[0][:, 0:n], lhsT=PWB[:, j, :], rhs=MC[:, j, PAD + c0:PAD + c0 + n], start=True, stop=True),
                         reads=[bPRM, bMC[tt][j]], writes=[bk[1]])
                    P.op("act", lambda e, bk=bk, j=j, c0=c0, n=n: e.activation(out=GT[:, 6 + j, c0:c0 + n], in_=bk[0][:, 0:n], func=AF.Copy,
                                                                              scale=PV[:, pb + 46 + j:pb + 47 + j]),
                         reads=[bk[1], bCONST], writes=[bGT[tt][6 + j]])

            P.cut(15 * l + 8)
            reg.switch()
            ca = Carve()
            MEMTOK = ca.take([1024], F32)
            MNB = ca.take([1024], BF16)
            SEL = ca.take([NS, 128], BF16)
            MTB = ca.take([8, 256], BF16)
            KT = ca.take([8, 256], BF16)
            VTOK = ca.take([2, 1024], BF16)
            PB_ = ca.take([2, 1024], BF16)
            PTS = ca.take([2, 1024], BF16)
            KS = ca.take([3, 2048], BF16)
            VS = KS
            QTOK = ca.take([1024], BF16)
            SS = ca.take([128], F32)
            PSM = ca.take([256], BF16)
            PTS2 = ca.take([2, 64], BF16)
            JUNK = ca.take([4, 256], BF16)
            bJUNK = [R() for _ in range(4)]
            jki = [0]
            bMEMTOK = [R()] * 2; bMNB = [R()] * 2; bMTB = R(); bSEL = R(); bKT = R(); bVTOK = R()
            bPB = [R(), R()]; bPTS = [R(), R()]; bKS = [R(), R(), R()]; bVS = bKS; bQTOK = R(); bSS = R(); bPSM = R(); bPTS2 = R()
            kssem = [P.newsem("ks0_%d" % l), P.newsem("ks1_%d" % l), P.newsem("ks2_%d" % l)]
            vssem = kssem
            msem = [P.newsem("mem0_%d" % l)] * 2

            so0 = load_block(w_out[l, :, 0:512])

            def gn(tt):
                for g in range(4):
                    rmsnorm(GT, bGT, 2 * g, 2, pb + 32 + 2 * g, GT, bGT, HT, bHT, tt, 1.0 / 256)

            P.cut(15 * l + 9)
            P.op("dve", lambda e: e.tensor_copy(out=SEL[0:NS, :, :], in_=IDB[0:NS, 0:NS].unsqueeze(2).broadcast_to([NS, NS, 128])), reads=[bCONST], writes=[bSEL])

            P.cut(15 * l + 9.1)

            def mem_tile(mt):
                P.op("sp", lambda e, mt=mt: e.dma_start(out=MEMTOK[:, :], in_=mem[mt * 128:(mt + 1) * 128, :]), writes=[bMEMTOK[mt]], sem=msem[mt])
                sc, bsc = stat_cols(2)
                P.op("dve", lambda e, sc=sc: e.memset(sc[:, 0:1], 0.0), writes=bsc)
                ti = nxt("tmp")
                P.op("dve", lambda e, mt=mt, sc=sc, ti=ti: e.scalar_tensor_tensor(out=TMP[:, ti, 0:512], in0=MEMTOK[:, 0:512], scalar=1.0, in1=MEMTOK[:, 0:512],
                                                                                  op0=ALU.mult, op1=ALU.mult, accum_out=sc[:, 0:1]),
                     reads=[bMEMTOK[mt]] + bsc, writes=[bTMP[ti]] + bsc)
                P.op("dve", lambda e, sc=sc: e.memset(sc[:, 1:2], 0.0), writes=bsc)
                ti = nxt("tmp")
                P.op("dve", lambda e, mt=mt, sc=sc, ti=ti: e.scalar_tensor_tensor(out=TMP[:, ti, 0:512], in0=MEMTOK[:, 512:1024], scalar=1.0, in1=MEMTOK[:, 512:1024],
                                                                                  op0=ALU.mult, op1=ALU.mult, accum_out=sc[:, 1:2]),
                     reads=[bMEMTOK[mt]] + bsc, writes=[bTMP[ti]] + bsc)
                P.op("dve", lambda e, sc=sc: e.tensor_tensor(out=sc[:, 0:1], in0=sc[:, 0:1], in1=sc[:, 1:2], op=ALU.add), reads=bsc, writes=bsc)
                P.op("act", lambda e, sc=sc: e.activation(out=sc[:, 0:1], in_=sc[:, 0:1], func=AF.Ln, bias=EPST[:, :], scale=1.0 / 1024), reads=bsc + [bCONST], writes=bsc)
                P.op("act", lambda e, sc=sc: e.activation(out=sc[:, 0:1], in_=sc[:, 0:1], func=AF.Exp, scale=-0.5), reads=bsc, writes=bsc)
                P.op("dve", lambda e, mt=mt, sc=sc: e.tensor_scalar(out=MNB[:, :], in0=MEMTOK[:, :], scalar1=sc[:, 0:1], scalar2=None, op0=ALU.mult),
                     reads=[bMEMTOK[mt]] + bsc, writes=[bMNB[mt]])
                for half in range(2):
                    bk = P.bank()
                    bkb = bk[0][:, :].bitcast(BF16)

                    def fn(e, bkb=bkb, mt=mt, half=half):
                        ins = None
                        for q in range(4):
                            f = half * 4 + q
                            ins = e.transpose(out=bkb[:, q * 128:(q + 1) * 128], in_=MNB[:, f * 128:(f + 1) * 128], identity=IDB[:, :])
                        return ins
                    P.op("pe", fn, reads=[bMNB[mt], bCONST], writes=[bk[1]])
                    for q in range(4):
                        f = half * 4 + q
                        P.op("dve", lambda e, bkb=bkb, q=q, f=f, mt=mt: e.tensor_scalar(out=MTB[:, f, mt * 128:(mt + 1) * 128], in0=bkb[:, q * 128:(q + 1) * 128],
                                                                                       scalar1=PV[:, pb + 24 + f:pb + 25 + f], scalar2=None, op0=ALU.mult),
                             reads=[bk[1], bCONST], writes=[bMTB])

            gn(0)
            mem_tile(0)
            for tt in range(5):
                if tt + 1 < 5:
                    gn(tt + 1)
                if tt == 1:
                    mem_tile(1)
                proj_block(so0, GT, bGT, evac_add_x(0), tts=[tt])
            so1 = load_block(w_out[l, :, 512:1024])
            proj_block(so1, GT, bGT, evac_add_x(4))
            P.cut(15 * l + 9.3)

            def kv_proj(slots, odram, dstT, bdstT, dsttok, bdsttok):
                for hb, s in enumerate(slots):
                    for mt in range(2):
                        bk = P.bank()

                        def fn(e, bk=bk, s=s, mt=mt):
                            ins = None
                            for k in range(8):
                                ins = e.matmul(bk[0][:, :], lhsT=MTB[:, k, mt * 128:(mt + 1) * 128], rhs=RING[:, s, k, :], start=(k == 0), stop=(k == 7))
                            return ins
                        P.op("pe", fn, reads=[bMTB, bRING[s]], writes=[bk[1]])
                        ti = nxt("tmp")
                        P.op("act", lambda e, bk=bk, ti=ti: e.activation(out=TMP[:, ti, :], in_=bk[0][:, :], func=AF.Copy), reads=[bk[1]], writes=[bTMP[ti]])
                        P.op("sp", lambda e, ti=ti, mt=mt, hb=hb: e.dma_start(out=odram[l, mt * 128:(mt + 1) * 128, hb * 512:(hb + 1) * 512], in_=TMP[:, ti, :]),
                             reads=[bTMP[ti]], sem=tmpsem[ti])
                        if dsttok is not None:
                            P.op("dve", lambda e, bk=bk, mt=mt, hb=hb: e.tensor_copy(out=dsttok[:, mt, hb * 512:(hb + 1) * 512], in_=bk[0][:, :]),
                                 reads=[bk[1]], writes=[bdsttok])
                    if dstT is not None:
                        for o in range(4):
                            bk = P.bank()

                            def fn2(e, bk=bk, s=s, o=o):
                                ins = None
                                for k in range(8):
                                    ins = e.matmul(bk[0][:, 0:256], lhsT=RING[:, s, k, o * 128:(o + 1) * 128], rhs=MTB[:, k, :], start=(k == 0), stop=(k == 7))
                                return ins
                            P.op("pe", fn2, reads=[bMTB, bRING[s]], writes=[bk[1]])
                            P.op("act", lambda e, bk=bk, o=o, hb=hb: e.activation(out=dstT[:, hb * 4 + o, :], in_=bk[0][:, 0:256], func=AF.Copy), reads=[bk[1]], writes=[bdstT])
            sk0 = load_block(w_xk[l, :, 0:512])
            sk1 = load_block(w_xk[l, :, 512:1024])
            kv_proj([sk0, sk1], o_mk, KT, bKT, None, None)
            P.cut(15 * l + 9.5)
            sv0 = load_block(w_xv[l, :, 0:512])
            sv1 = load_block(w_xv[l, :, 512:1024])
            kv_proj([sv0, sv1], o_mv, None, None, VTOK, bVTOK)

            P.cut(15 * l + 10)
            def evac_q(obase):
                def f(o, tt, bk, bb):
                    c0, n = TTS[tt]
                    P.op("act", lambda e: e.activation(out=GT[:, obase + o, c0:c0 + n], in_=bk[:, 0:n], func=AF.Copy, scale=1.0 / 16),
                         reads=[bb], writes=[bGT[tt][obase + o]])
                return f
            sq0 = load_block(w_xq[l, :, 0:512])
            rmsnorm(XT, bXT, 0, 8, pb + 8, HT, bHT, HT, bHT, 0, 1.0 / 1024)
            for tt in range(5):
                if tt + 1 < 5:
                    rmsnorm(XT, bXT, 0, 8, pb + 8, HT, bHT, HT, bHT, tt + 1, 1.0 / 1024)
                proj_block(sq0, HT, bHT, evac_q(0), tts=[tt])
            sq1 = load_block(w_xq[l, :, 512:1024])
            proj_block(sq1, HT, bHT, evac_q(4))
            so0 = load_block(w_xo[l, :, 0:512])

            P.cut(15 * l + 11)
            qb = P.bank()
            qbb = qb[0][:, :].bitcast(BF16)

            def fnq(e):
                ins = None
                for o in range(8):
                    ins = e.transpose(out=qbb[0:NS, o * 128:(o + 1) * 128], in_=GT[:, o, NT:NCOL], identity=IDB[:, :])
                return ins
            P.op("pe", fnq, reads=bGT[4] + [bCONST], writes=[qb[1]])
            P.op("act", lambda e: e.activation(out=QTOK[0:NS, :], in_=qbb[0:NS, :], func=AF.Copy), reads=[qb[1]], writes=[bQTOK])
            bSSc = [R() for _ in range(128)]
            KST = {}

            def samp_k_pre(b):
                sl = b % 3
                P.op("pool", lambda e: e.dma_start(out=KS[:, sl, :].rearrange("p (t c) -> p t c", t=2), in_=ck[l, b].rearrange("(t p) c -> p t c", p=128)),
                     writes=[bKS[sl]], sem=kssem[sl])

            def samp_k_half(b, half):
                sl = b % 3
                if b == 0 and half == 0:
                    P.op("dve", lambda e: e.memset(SS[:, :], 0.0), writes=bSSc)
                qh = P.bank()
                P.op("pe", lambda e: e.matmul(qh[0][:, :], lhsT=SEL[0:NS, b, :], rhs=QTOK[0:NS, half * 512:(half + 1) * 512], start=True, stop=True),
                     reads=[bQTOK, bSEL], writes=[qh[1]])
                for mt in range(2):
                    for hl in range(2):
                        h = 2 * half + hl
                        col = mt * 64 + b * 4 + h
                        jk = jki[0]
                        jki[0] = (jk + 1) % 4
                        P.op("dve", lambda e, mt=mt, h=h, hl=hl, col=col, jk=jk: e.scalar_tensor_tensor(
                            out=JUNK[:, jk, :], in0=KS[:, sl, mt * 1024 + h * 256:mt * 1024 + h * 256 + 256], scalar=1.0,
                            in1=qh[0][:, hl * 256:hl * 256 + 256], op0=ALU.mult, op1=ALU.mult, accum_out=SS[:, col:col + 1]),
                            reads=[bKS[sl], qh[1]], writes=[bSSc[col], bJUNK[jk]])

            AST = {}

            def attn_A(ti_):
                tt = ti_ // 4
                c0 = ti_ * 128
                sb0 = P.bank(); i0 = P.last_idx; P.reserved.add(i0)
                sb1 = P.bank(); i1 = P.last_idx; P.reserved.add(i1)
                sbs = [sb0, sb1]
                AST[ti_] = (sbs, i0, i1)

                def fns(e):
                    ins = None
                    for h in range(4):
                        for dh in range(2):
                            ins = e.matmul(sbs[h // 2][0][:, (h % 2) * 256:(h % 2) * 256 + 256], lhsT=GT[:, 2 * h + dh, c0:c0 + 128], rhs=KT[:, 2 * h + dh, :],
                                           start=(dh == 0), stop=(dh == 1))
                    return ins
                P.op("pe", fns, reads=bGT[tt] + [bKT], writes=[sb0[1], sb1[1]])

            BST = {}

            def attn_B1(ti_):
                sl = ti_ % 2
                sbs, i0, i1 = AST[ti_]
                P.reserved.discard(i0); P.reserved.discard(i1)
                mx, bmx = stat_cols(12)
                bm = [bmx[0]]
                bs_ = [bmx[1]]
                for hp in range(2):
                    P.op("dve", lambda e, hp=hp: e.tensor_reduce(out=mx[:, 2 * hp:2 * hp + 2], in_=sbs[hp][0][:, :].rearrange("p (h m) -> p h m", h=2),
                                                                 axis=AX.X, op=ALU.max, negate=True), reads=[sbs[hp][1]], writes=bm)
                for h in range(4):
                    P.op("act", lambda e, h=h: e.activation(out=PB_[:, sl, h * 256:(h + 1) * 256], in_=sbs[h // 2][0][:, (h % 2) * 256:(h % 2) * 256 + 256],
                                                            func=AF.Exp, bias=mx[:, h:h + 1], accum_out=mx[:, 8 + h:9 + h]),
                         reads=[sbs[h // 2][1]] + bm, writes=[bPB[sl]] + bs_)
                BST[ti_] = (mx, bm, bs_)

            def attn_B(ti_):
                tt = ti_ // 4
                c0 = ti_ * 128
                sl = ti_ % 2
                mx, bm, bs_ = BST[ti_]
                P.op("dve", lambda e: e.reciprocal(out=mx[:, 4:8], in_=mx[:, 8:12]), reads=bs_, writes=bm)
                P.op("dve", lambda e: e.tensor_tensor(out=PB_[:, sl, :].rearrange("p (h m) -> p h m", h=4), in0=PB_[:, sl, :].rearrange("p (h m) -> p h m", h=4),
                                                      in1=mx[:, 4:8].unsqueeze(2).broadcast_to([128, 4, 256]), op=ALU.mult),
                     reads=[bPB[sl]] + bm, writes=[bPB[sl]])
                samp_k_half(ti_, 0)
                samp_k_half(ti_, 1)
                tb = P.bank()
                tbb = tb[0][:, :].bitcast(BF16)

                def fnt(e):
                    ins = None
                    for q in range(8):
                        ins = e.transpose(out=tbb[:, q * 128:(q + 1) * 128], in_=PB_[:, sl, q * 128:(q + 1) * 128], identity=IDB[:, :])
                    return ins
                P.op("pe", fnt, reads=[bPB[sl], bCONST], writes=[tb[1]])
                P.op("act", lambda e: e.activation(out=PTS[:, sl, :], in_=tbb[:, :], func=AF.Copy), reads=[tb[1]], writes=[bPTS[sl]])
                ob0 = P.bank(); ob1 = P.bank()
                obs = [ob0, ob1]

                def fno(e):
                    ins = None
                    for h in range(4):
                        for dh in range(2):
                            q = 2 * h + dh
                            for mt in range(2):
                                ins = e.matmul(obs[q // 4][0][:, (q % 4) * 128:(q % 4) * 128 + 128], lhsT=VTOK[:, mt, h * 256 + dh * 128:h * 256 + dh * 128 + 128],
                                               rhs=PTS[:, sl, (2 * h + mt) * 128:(2 * h + mt) * 128 + 128], start=(mt == 0), stop=(mt == 1))
                    return ins
                P.op("pe", fno, reads=[bVTOK, bPTS[sl]], writes=[ob0[1], ob1[1]])
                for hh in range(2):
                    P.op("act", lambda e, hh=hh: e.activation(out=HT[:, hh * 4:hh * 4 + 4, c0:c0 + 128], in_=obs[hh][0][:, :].rearrange("p (q t) -> p q t", q=4), func=AF.Copy),
                         reads=[obs[hh][1]], writes=bHT[tt][hh * 4:hh * 4 + 4])

            attn_A(0)
            attn_A(1)
            samp_k_pre(0)
            attn_B1(0)
            for ti_ in range(16):
                if ti_ + 2 < 16:
                    attn_A(ti_ + 2)
                if ti_ + 1 < 16:
                    samp_k_pre(ti_ + 1)
                    attn_B1(ti_ + 1)
                attn_B(ti_)
            P.cut(15 * l + 12)

            stb = P.bank()

            def fnst(e):
                ins = None
                for mt in range(2):
                    ins = e.transpose(out=stb[0][0:64, mt * 128:(mt + 1) * 128], in_=SS[:, mt * 64:(mt + 1) * 64], identity=IDF[:, :])
                return ins
            P.op("pe", fnst, reads=bSSc + [bCONST], writes=[stb[1]])
            mx, bmx = stat_cols(4)
            P.op("dve", lambda e: e.tensor_reduce(out=mx[0:64, 1:2], in_=stb[0][0:64, 0:256], axis=AX.X, op=ALU.max, negate=True), reads=[stb[1]], writes=bmx)
            P.op("act", lambda e: e.activation(out=PSM[0:64, :], in_=stb[0][0:64, 0:256], func=AF.Exp, bias=mx[0:64, 1:2], accum_out=mx[0:64, 2:3]),
                 reads=[stb[1]] + bmx, writes=[bPSM] + bmx)
            P.op("dve", lambda e: e.reciprocal(out=mx[0:64, 3:4], in_=mx[0:64, 2:3]), reads=bmx, writes=bmx)
            P.op("dve", lambda e: e.tensor_scalar(out=PSM[0:64, :], in0=PSM[0:64, :], scalar1=mx[0:64, 3:4], scalar2=None, op0=ALU.mult), reads=[bPSM] + bmx, writes=[bPSM])
            ptb = P.bank()
            ptbb = ptb[0][:, :].bitcast(BF16)

            def fnpt(e):
                ins = None
                for mt in range(2):
                    ins = e.transpose(out=ptbb[:, mt * 64:(mt + 1) * 64], in_=PSM[0:64, mt * 128:(mt + 1) * 128], identity=IDB[0:64, 0:64])
                return ins
            P.op("pe", fnpt, reads=[bPSM, bCONST], writes=[ptb[1]])
            P.op("act", lambda e: e.activation(out=PTS2[:, :, :], in_=ptbb[:, 0:128].rearrange("p (t c) -> p t c", t=2), func=AF.Copy), reads=[ptb[1]], writes=[bPTS2])
            osb = P.bank()
            osb_idx = P.last_idx
            P.reserved.add(osb_idx)

            def samp_v(b):
                sl = b % 3
                P.op("pool", lambda e, b=b, sl=sl: e.dma_start(out=VS[:, sl, :].rearrange("p (t c) -> p t c", t=2), in_=cv[l, b].rearrange("(t p) c -> p t c", p=128)),
                     writes=[bVS[sl]], sem=vssem[sl])

                def fnv2(e, b=b, sl=sl):
                    ins = None
                    for h in range(4):
                        for dh in range(2):
                            q = 2 * h + dh
                            for mt in range(2):
                                ins = e.matmul(osb[0][:, q * NS + b:q * NS + b + 1], lhsT=VS[:, sl, mt * 1024 + h * 256 + dh * 128:mt * 1024 + h * 256 + dh * 128 + 128],
                                               rhs=PTS2[:, mt, b * 4 + h:b * 4 + h + 1], start=(mt == 0), stop=(mt == 1))
                    return ins
                P.op("pe", fnv2, reads=[bVS[sl], bPTS2], writes=[osb[1]])

            P.cut(15 * l + 13)
            so1 = load_block(w_xo[l, :, 512:1024])
            for tt in range(4):
                proj_block(so0, HT, bHT, evac_add_x(0), tts=[tt])
                samp_v(2 * tt); samp_v(2 * tt + 1)
            for tt in range(4):
                proj_block(so1, HT, bHT, evac_add_x(4), tts=[tt])
                samp_v(8 + 2 * tt); samp_v(9 + 2 * tt)
            P.op("act", lambda e: e.activation(out=HT[:, :, NT:NCOL], in_=osb[0][:, 0:8 * NS].rearrange("p (q b) -> p q b", q=8), func=AF.Copy),
                 reads=[osb[1]], writes=bHT[4])
            P.reserved.discard(osb_idx)
            proj_block(so0, HT, bHT, evac_add_x(0), tts=[4])
            proj_block(so1, HT, bHT, evac_add_x(4), tts=[4])

            P.cut(15 * l + 14)
            reg.switch()

            def evac_ff1(obase):
                def f(o, tt, bk, bb):
                    c0, n = TTS[tt]
                    ti = nxt("tmp")
                    P.op("act", lambda e: e.activation(out=TMP[:, ti, 0:n], in_=bk[:, 0:n], func=AF.Relu), reads=[bb], writes=[bTMP[ti]])
                    P.op("dve", lambda e: e.tensor_tensor(out=GT[:, obase + o, c0:c0 + n], in0=TMP[:, ti, 0:n], in1=TMP[:, ti, 0:n], op=ALU.mult),
                         reads=[bTMP[ti]], writes=[bGT[tt][obase + o]])
                return f
            for c in range(4):
                sa = load_block(w_ff1[l, :, c * 1024:c * 1024 + 512])
                if c == 0:
                    rmsnorm(XT, bXT, 0, 8, pb + 16, HT, bHT, HT, bHT, 0, 1.0 / 1024)
                    for tt in range(5):
                        if tt + 1 < 5:
                            rmsnorm(XT, bXT, 0, 8, pb + 16, HT, bHT, HT, bHT, tt + 1, 1.0 / 1024)
                        proj_block(sa, HT, bHT, evac_ff1(0), tts=[tt])
                else:
                    proj_block(sa, HT, bHT, evac_ff1(0))
                sb_ = load_block(w_ff1[l, :, c * 1024 + 512:c * 1024 + 1024])
                proj_block(sb_, HT, bHT, evac_ff1(4))
                sc_ = load_block(w_ff2[l, c * 1024:(c + 1) * 1024, 0:512])
                proj_block(sc_, GT, bGT, evac_add_x(0))
                sd_ = load_block(w_ff2[l, c * 1024:(c + 1) * 1024, 512:1024])
                proj_block(sd_, GT, bGT, evac_add_x(4))

        for l in range(L):
            layer(l)
            P.cut(15 * l + 15)

        reg.switch()
        cf = Carve()
        YF2 = cf.take([2, 8, 512], F32)
        OST = cf.take([2, 1024], F32)
        bYF2 = [[Buf(reg) for _ in range(8)] for _ in range(2)]
        bOST = [Buf(reg), Buf(reg)]
        ostsem = [P.newsem("ost0"), P.newsem("ost1")]
        FG = 2 * PL
        oi_ = [0]

        def final_tile(tt):
            c0, n = TTS[tt]
            YF = YF2[:, tt % 2]
            bYF = bYF2[tt % 2]
            P.op("act", lambda e, c0=c0, n=n: e.activation(out=HT[:, :, c0:c0 + n], in_=XT[:, :, c0:c0 + n], func=AF.Square), reads=bXT[tt], writes=bHT[tt])
            bk = P.bank()

            def fn(e, bk=bk, c0=c0, n=n):
                ins = None
                for k in range(8):
                    ins = e.matmul(bk[0][:, 0:n], lhsT=ONESB[:, :], rhs=HT[:, k, c0:c0 + n], start=(k == 0), stop=(k == 7))
                return ins
            P.op("pe", fn, reads=bHT[tt] + [bCONST], writes=[bk[1]])
            ri = rstd_from_bank(bk, n, 1.0 / 1024)
            for k in range(8):
                P.op("dve", lambda e, k=k, ri=ri, c0=c0, n=n: e.scalar_tensor_tensor(out=YF[:, k, 0:n], in0=XT[:, k, c0:c0 + n], scalar=PV[:, FG + k:FG + k + 1],
                                                                                    in1=RS[:, ri, 0:n], op0=ALU.mult, op1=ALU.mult),
                     reads=[bXT[tt][k], bRS[ri], bCONST], writes=[bYF[k]])
            nch = 4 if tt < 4 else 1
            np_ = 128 if tt < 4 else NS
            for c in range(nch):
                sl = oi_[0] % 2
                oi_[0] += 1
                for half in range(2):
                    bk2 = P.bank()

                    def fn2(e, bk2=bk2, c=c, half=half, np_=np_):
                        ins = None
                        for q in range(4):
                            f = half * 4 + q
                            ins = e.transpose(out=bk2[0][0:np_, q * 128:(q + 1) * 128], in_=YF[:, f, c * 128:c * 128 + np_], identity=IDF[:, :])
                        return ins
                    P.op("pe", fn2, reads=bYF[half * 4:half * 4 + 4] + [bCONST], writes=[bk2[1]])
                    if half == 0:
                        P.op("act", lambda e, bk2=bk2, sl=sl, np_=np_: e.activation(out=OST[0:np_, sl, 0:512], in_=bk2[0][0:np_, :], func=AF.Copy), reads=[bk2[1]], writes=[bOST[sl]])
                    else:
                        P.op("dve", lambda e, bk2=bk2, sl=sl, np_=np_: e.tensor_copy(out=OST[0:np_, sl, 512:1024], in_=bk2[0][0:np_, :]), reads=[bk2[1]], writes=[bOST[sl]])
                dst = o_yp[c0 + c * 128:c0 + (c + 1) * 128, :] if tt < 4 else o_ys[:, :]
                P.op("sp", lambda e, sl=sl, np_=np_, dst=dst: e.dma_start(out=dst, in_=OST[0:np_, sl, :]), reads=[bOST[sl]], sem=ostsem[sl])

        for tt in range(5):
            final_tile(tt)

        with nc.Block() as block:
            @block.tensor
            def _(e):
                P.emit("pe", e)

            @block.scalar
            def _(e):
                P.emit("act", e)

            @block.vector
            def _(e):
                P.emit("dve", e)

            @block.gpsimd
            def _(e):
                P.emit("pool", e)

            @block.sync
            def _(e):
                P.emit("sp", e)
                P.final_waits(e)
    return nc


_NC = None


def _fm(v):
    v = np.asarray(v, np.float32)
    return np.ascontiguousarray(v.reshape(-1, 128).T)


def _host_params(inp):
    pv = np.zeros((128, NPV), np.float32)
    for l in range(L):
        b = l * PL
        pv[:, b + 0:b + 8] = _fm(inp["norm_mix"][l])
        pv[:, b + 8:b + 16] = _fm(inp["norm_xattn"][l])
        pv[:, b + 16:b + 24] = _fm(inp["norm_ffn"][l])
        pv[:, b + 24:b + 32] = _fm(inp["norm_mem"][l])
        pv[:, b + 32:b + 40] = _fm(inp["mix_out_g"][l])
        pv[:, b + 40:b + 42] = _fm(inp["conf_dw_b"][l])
        pv[:, b + 42:b + 44] = _fm(inp["conf_ln_g"][l])
        pv[:, b + 44:b + 46] = _fm(inp["conf_ln_b"][l])
        pv[:, b + 46:b + 48] = _fm(inp["pool_scale"][l])
        pv[:, b + 48:b + 50] = _fm(np.repeat(np.asarray(inp["gmlp_ws"])[l, :, 0, 0], 64))
        pv[:, b + 50:b + 52] = _fm(np.repeat(np.asarray(inp["gmlp_bs"])[l, :, 0], 64))
        cdw = np.asarray(inp["conf_dw"])[l]
        for j in range(2):
            pv[:, b + 52 + j * 31:b + 52 + (j + 1) * 31] = cdw[:, j * 128:(j + 1) * 128].T
        sdw = np.asarray(inp["sc_dw"])[l]
        for j in range(2):
            pv[:, b + 114 + j * 3:b + 114 + (j + 1) * 3] = sdw[:, j * 128:(j + 1) * 128].T
    g = 2 * PL
    pv[:, g:g + 8] = _fm(inp["norm_final"])
    wins = np.repeat(np.array([2, 4, 8, 16], np.float32), 64)
    for j in range(2):
        w = wins[j * 128:(j + 1) * 128]
        for k in range(16):
            pv[:, g + 8 + j * 16 + k] = np.where(k < w, 1.0 / w, 0.0) - (1.0 if k == 0 else 0.0)
        for t in range(16):
            pv[:, g + 40 + j * 16 + t] = w / np.minimum(w, t + 1.0)
        for kk in range(15):
            k = 15 - kk
            pv[:, g + 72 + j * 16 + kk] = np.where(k < w, 1.0 / w, 0.0)
    bc = np.zeros((L, 128, 512), np.float32)
    wst = np.zeros((L, 128, 512), np.float32)
    pwb = np.zeros((L, 128, 256), np.float32)
    bsrow = np.zeros((L, 1, 512), np.float32)
    for l in range(L):
        bc[l, :, 0:256] = np.asarray(inp["gmlp_ln_g"])[l][None, :]
        bc[l, :, 256:512] = np.asarray(inp["gmlp_ln_b"])[l][None, :]
        ws = np.asarray(inp["gmlp_ws"])[l]
        wst[l] = np.transpose(ws, (2, 0, 1)).reshape(128, 512)
        pw = np.asarray(inp["pool_w"])[l]
        for j in range(2):
            for gg in range(2):
                pwb[l, gg * 64:(gg + 1) * 64, j * 128 + gg * 64:j * 128 + (gg + 1) * 64] = pw[2 * j + gg]
        bsrow[l, 0] = np.asarray(inp["gmlp_bs"])[l].reshape(512)
    mask = np.triu(np.ones((128, 128), np.float32))
    idf = np.eye(128, dtype=np.float32)
    return dict(pv=pv, bc=bc, wst=wst, pwb=pwb, bsrow=bsrow, mask=mask, idf=idf)


def kernel(**inp):
    global _NC
    inp = {k: np.asarray(v) for k, v in inp.items()}
    if _NC is None:
        _NC = build_nc()
    nc = _NC
    hp = _host_params(inp)
    shared = {k: np.ascontiguousarray(inp[k], dtype=np.float32) for k in
              ("w_in", "w_out", "w_xq", "w_xk", "w_xv", "w_xo", "w_ff1", "w_ff2")}
    shared.update(hp)
    in_maps = []
    for c in range(8):
        b0 = c * NS
        m = dict(shared)
        m["xp"] = np.ascontiguousarray(inp["x_prompt"][c])
        m["xs"] = np.ascontiguousarray(inp["x_sample"][b0:b0 + NS, 0, :])
        m["mem"] = np.ascontiguousarray(inp["mem_prompt"][c])
        m["ck"] = np.ascontiguousarray(inp["cache_mem_k"][:, b0:b0 + NS].reshape(L, NS, 256, 1024))
        m["cv"] = np.ascontiguousarray(inp["cache_mem_v"][:, b0:b0 + NS].reshape(L, NS, 256, 1024))
        m["sglu"] = np.ascontiguousarray(inp["state_conv_glu"][:, b0:b0 + NS].reshape(L, NS * 30, 256))
        m["ssh"] = np.ascontiguousarray(inp["state_conv_short"][:, b0:b0 + NS].reshape(L, NS * 2, 256))
        m["spl"] = np.ascontiguousarray(inp["state_pool"][:, b0:b0 + NS].reshape(L, NS * 15, 256))
        in_maps.append(m)
    res = run_bass_kernel_spmd(nc, in_maps, core_ids=list(range(8)))
    R = res.results
    cat = lambda k, ax: np.concatenate([np.asarray(r[k]) for r in R], axis=ax)
    y_prompt = np.stack([np.asarray(r["o_yp"]) for r in R], 0).astype(np.float32)
    y_sample = cat("o_ys", 0).reshape(128, 1, 1024).astype(np.float32)
    mk = np.stack([np.asarray(r["o_mk"]) for r in R], 1).reshape(L, 8, 256, 4, 256).astype(np.float32)
    mv = np.stack([np.asarray(r["o_mv"]) for r in R], 1).reshape(L, 8, 256, 4, 256).astype(np.float32)
    glp = np.stack([np.asarray(r["o_glp"]) for r in R], 1).astype(np.float32)
    gls = np.concatenate([np.asarray(r["o_gls"]).reshape(L, NS, 30, 256) for r in R], 1).astype(np.float32)
    shp = np.stack([np.asarray(r["o_shp"]) for r in R], 1).astype(np.float32)
    shs = np.concatenate([np.asarray(r["o_shs"]).reshape(L, NS, 2, 256) for r in R], 1).astype(np.float32)
    plp = np.stack([np.asarray(r["o_plp"]) for r in R], 1).astype(np.float32)
    pls = np.concatenate([np.asarray(r["o_pls"]).reshape(L, NS, 15, 256) for r in R], 1).astype(np.float32)
    gv = np.concatenate([np.asarray(r["o_gv"]).reshape(L, NS, 1, 256) for r in R], 1).astype(np.float32)
    return (y_prompt, y_sample, mk, mv, glp, gls, shp, shs, plp, pls, gv)
```

```python
import contextlib
import os
import numpy as np
import concourse.bass as bass
import concourse.mybir as mybir
from concourse.bass_utils import run_bass_kernel_spmd

F32 = mybir.dt.float32
BF16 = mybir.dt.bfloat16
AF = mybir.ActivationFunctionType
ALU = mybir.AluOpType
AX = mybir.AxisListType

L = 2
NT = 2048
NS = 16
NCOL = NT + NS
TTS = [(0, 512), (512, 512), (1024, 512), (1536, 512), (2048, 16)]
PAD = 32
EPS = 1e-6
PL = 120
NPV = 2 * PL + 104
NRING = 2


class Buf:
    __slots__ = ("w", "r", "reg", "excl", "pending")

    def __init__(self, reg=None, excl=False):
        self.w = None
        self.r = {}
        self.reg = reg
        self.pending = False
        self.excl = excl


class Region:
    def __init__(self):
        self.prev = {}
        self.cur = {}

    def switch(self):
        for s, v in self.cur.items():
            if self.prev.get(s, 0) < v:
                self.prev[s] = v
        self.cur = {}


class Prog:
    ENG = ("pe", "act", "dve", "pool", "sp")

    def __init__(self, nc, stack):
        self.nc = nc
        self.stack = stack
        self.eng = {"pe": nc.tensor, "act": nc.scalar, "dve": nc.vector, "pool": nc.gpsimd, "sp": nc.sync}
        self.ops = {e: [] for e in self.ENG}
        self.semh = {}
        self.cnt = {}
        for e in self.ENG:
            self.newsem(e)
        self.banks = []
        self.bi = 0
        self.enabled = True
        self.reserved = set()
        self.last_idx = 0
        self.cutat = 1e9

    def cut(self, k):
        if k >= self.cutat:
            self.enabled = False

    def newsem(self, name):
        self.semh[name] = self.stack.enter_context(self.nc.semaphore("s_" + name))
        self.cnt[name] = 0
        return name

    def op(self, eng, fn, reads=(), writes=(), sem=None):
        if not self.enabled:
            return None
        deps = {}

        def add(s, v):
            if deps.get(s, 0) < v:
                deps[s] = v

        own = eng if sem is None else sem
        for b in reads:
            if b.w is not None:
                add(*b.w)
            if b.excl:
                for s, v in b.r.items():
                    if s != own:
                        add(s, v)
        for b in writes:
            if b.w is not None:
                add(*b.w)
            for s, v in b.r.items():
                add(s, v)
        for b in list(reads) + list(writes):
            if b.reg is not None:
                for s, v in b.reg.prev.items():
                    add(s, v)
        if sem is None:
            sname, inc = eng, 1
        else:
            sname, inc = sem, 16
        self.cnt[sname] += inc
        h = (sname, self.cnt[sname])
        self.ops[eng].append((deps, fn, h, inc))
        for b in reads:
            if b.r.get(sname, 0) < h[1]:
                b.r[sname] = h[1]
        for b in writes:
            b.w = h
            b.r = {}
            b.pending = True
        for b in reads:
            b.pending = False
        for b in list(reads) + list(writes):
            if b.reg is not None and b.reg.cur.get(sname, 0) < h[1]:
                b.reg.cur[sname] = h[1]
        return h

    def emit(self, ename, e):
        waited = {}
        for deps, fn, h, inc in self.ops[ename]:
            for s, v in deps.items():
                if s == "pe" and ename == "pe":
                    continue
                if waited.get(s, 0) >= v:
                    continue
                e.wait_ge(self.semh[s], v)
                waited[s] = v
            ins = fn(e)
            ins.then_inc(self.semh[h[0]], inc)

    def final_waits(self, e):
        for s, v in self.cnt.items():
            if v > 0:
                e.wait_ge(self.semh[s], v)

    def bank(self):
        i = self.bi
        n = 0
        while (i in self.reserved or self.banks[i][1].pending) and n < 8:
            i = (i + 1) % 8
            n += 1
        assert n < 8, "no free PSUM bank"
        self.bi = (i + 1) % 8
        self.last_idx = i
        return self.banks[i]


def build_nc():
    nc = bass.Bass("TRN2", target_bir_lowering=False)
    di = lambda n, s: nc.dram_tensor(n, s, F32, kind="ExternalInput").ap()
    do = lambda n, s: nc.dram_tensor(n, s, F32, kind="ExternalOutput").ap()
    xp = di("xp", [NT, 1024]); xs = di("xs", [NS, 1024]); mem = di("mem", [256, 1024])
    ck = di("ck", [L, NS, 256, 1024]); cv = di("cv", [L, NS, 256, 1024])
    sglu = di("sglu", [L, NS * 30, 256]); ssh = di("ssh", [L, NS * 2, 256]); spl = di("spl", [L, NS * 15, 256])
    w_in = di("w_in", [L, 1024, 2048]); w_out = di("w_out", [L, 1024, 1024])
    w_xq = di("w_xq", [L, 1024, 1024]); w_xk = di("w_xk", [L, 1024, 1024]); w_xv = di("w_xv", [L, 1024, 1024])
    w_xo = di("w_xo", [L, 1024, 1024]); w_ff1 = di("w_ff1", [L, 1024, 4096]); w_ff2 = di("w_ff2", [L, 4096, 1024])
    pvd = di("pv", [128, NPV]); bcd = di("bc", [L, 128, 512]); wstd = di("wst", [L, 128, 512])
    maskd = di("mask", [128, 128]); idfd = di("idf", [128, 128]); pwbd = di("pwb", [L, 128, 256])
    bsrd = di("bsrow", [L, 1, 512])
    o_yp = do("o_yp", [NT, 1024]); o_ys = do("o_ys", [NS, 1024])
    o_mk = do("o_mk", [L, 256, 1024]); o_mv = do("o_mv", [L, 256, 1024])
    o_glp = do("o_glp", [L, 30, 256]); o_gls = do("o_gls", [L, NS * 30, 256])
    o_shp = do("o_shp", [L, 2, 256]); o_shs = do("o_shs", [L, NS * 2, 256])
    o_plp = do("o_plp", [L, 15, 256]); o_pls = do("o_pls", [L, NS * 15, 256])
    o_gv = do("o_gv", [L, NS, 256])

    with contextlib.ExitStack() as st:
        P = Prog(nc, st)
        sb = lambda n, s, d: st.enter_context(nc.sbuf_tensor(n, s, d))
        XT = sb("XT", [128, 8, NCOL], F32)
        HT = sb("HT", [128, 8, NCOL], BF16)
        GT = sb("GT", [128, 8, NCOL], BF16)
        RING = sb("RING", [128, NRING, 8, 512], BF16)
        RS = sb("RS", [128, 2, 512], F32)
        TMP = sb("TMP", [128, 2, 512], F32)
        IDF = sb("IDF", [128, 128], F32)
        IDB = sb("IDB", [128, 128], BF16)
        ONESB = sb("ONESB", [128, 128], BF16)
        PV = sb("PV", [128, NPV], F32)
        EPST = sb("EPST", [128, 1], F32)
        STAT = sb("STAT", [128, 64], F32)
        SHW = 48 * 1024 // 2
        SH = sb("SH", [128, SHW], BF16)
        for i in range(8):
            P.banks.append((st.enter_context(nc.psum_tensor("bank%d" % i, [128, 512], F32)), Buf(excl=True)))

        reg = Region()

        class Carve:
            def __init__(self):
                self.off = 0

            def take(self, shape, dt):
                n = int(np.prod(shape))
                nb = n * (2 if dt == BF16 else 4)
                nb = (nb + 31) // 32 * 32
                ap = SH[:, self.off // 2:(self.off + nb) // 2]
                if dt == F32:
                    ap = ap.bitcast(F32)[:, 0:n]
                else:
                    ap = ap[:, 0:n]
                self.off += nb
                assert self.off <= SHW * 2, self.off
                if len(shape) == 2:
                    ap = ap.rearrange("p (a b) -> p a b", a=shape[0])
                elif len(shape) == 3:
                    ap = ap.rearrange("p (a b c) -> p a b c", a=shape[0], b=shape[1])
                return ap

        bXT = [[Buf() for _ in range(8)] for _ in TTS]
        bHT = [[Buf() for _ in range(8)] for _ in TTS]
        bGT = [[Buf() for _ in range(8)] for _ in TTS]
        bRING = [Buf() for _ in range(NRING)]
        bRS = [Buf(), Buf()]
        bTMP = [Buf(), Buf()]
        bCONST = Buf()
        rot = {"rs": 0, "tmp": 0, "ring": 0, "stat": 0}
        ringsem = [P.newsem("ring%d" % i) for i in range(NRING)]
        P.newsem("prm"); P.newsem("outs"); P.newsem("ldm"); P.newsem("pq"); P.newsem("pq0"); P.newsem("gvs")
        toutsem = [P.newsem("tout0"), P.newsem("tout1")]
        tmpsem = [P.newsem("tmpo0"), P.newsem("tmpo1")]

        def nxt(k, n=2):
            i = rot[k]
            rot[k] = (i + 1) % n
            return i

        def stat_cols(n):
            g = (n + 7) // 8
            i = rot["stat"]
            if i + g > 8:
                i = 0
            rot["stat"] = (i + g) % 8
            return STAT[:, i * 8:i * 8 + n], [bSTAT[i + q] for q in range(g)]

        bSTAT = [Buf() for _ in range(8)]

        P.op("sp", lambda e: e.dma_start(out=PV[:], in_=pvd[:, :]), writes=[bCONST], sem="prm")
        P.op("sp", lambda e: e.dma_start(out=IDF[:], in_=idfd[:, :]), writes=[bCONST], sem="prm")
        P.op("pool", lambda e: e.dma_start(out=IDB[:], in_=idfd[:, :]), writes=[bCONST], sem="pq0")
        P.op("dve", lambda e: e.memset(ONESB[:], 1.0), writes=[bCONST])
        P.op("dve", lambda e: e.memset(EPST[:], EPS), writes=[bCONST])

        P.cut(0.2)
        def load_block(src_ap):
            s = nxt("ring", NRING)
            assert not bRING[s].pending, "ring slot reloaded before its previous content was consumed"
            P.op("pool", lambda e, s=s: e.dma_start(out=RING[:, s], in_=src_ap.rearrange("(k p) n -> p k n", p=128)),
                 writes=[bRING[s]], sem=ringsem[s])
            return s

        def proj_block(s, in_t, b_in, evac, tts=range(5), otiles=range(4), nk=8):
            for tt in tts:
                c0, n = TTS[tt]
                if n <= 64:
                    shared = P.bank()
                    bks = [(shared[0][:, oi * n:(oi + 1) * n], shared[1]) for oi, _ in enumerate(otiles)]
                    wr = [shared[1]]
                else:
                    bks = [P.bank() for _ in otiles]
                    wr = [b[1] for b in bks]

                def fn(e, bks=bks, c0=c0, n=n):
                    ins = None
                    for oi, o in enumerate(otiles):
                        for k in range(nk):
                            ins = e.matmul(bks[oi][0][:, 0:n], lhsT=RING[:, s, k, o * 128:(o + 1) * 128],
                                           rhs=in_t[:, k, c0:c0 + n], start=(k == 0), stop=(k == nk - 1))
                    return ins
                P.op("pe", fn, reads=[bRING[s]] + b_in[tt][0:nk], writes=wr)
                for oi, o in enumerate(otiles):
                    evac(o, tt, bks[oi][0], bks[oi][1])

        def rstd_from_bank(bk, n, scale, np_=128):
            i = nxt("rs")
            P.op("act", lambda e: e.activation(out=RS[0:np_, i, 0:n], in_=bk[0][0:np_, 0:n], func=AF.Ln,
                                               bias=EPST[0:np_, :], scale=scale),
                 reads=[bk[1], bCONST], writes=[bRS[i]])
            P.op("act", lambda e: e.activation(out=RS[0:np_, i, 0:n], in_=RS[0:np_, i, 0:n], func=AF.Exp, scale=-0.5),
                 reads=[bRS[i]], writes=[bRS[i]])
            return i

        def rmsnorm(src, bsrc, k0, nk, gcol, dst, bdst, scr, bscr, tt, inv_n):
            c0, n = TTS[tt]
            P.op("act", lambda e: e.activation(out=scr[:, k0:k0 + nk, c0:c0 + n], in_=src[:, k0:k0 + nk, c0:c0 + n],
                                               func=AF.Square),
                 reads=bsrc[tt][k0:k0 + nk], writes=bscr[tt][k0:k0 + nk])
            bk = P.bank()

            def fn(e):
                ins = None
                for k in range(nk):
                    ins = e.matmul(bk[0][:, 0:n], lhsT=ONESB[:, :], rhs=scr[:, k0 + k, c0:c0 + n],
                                   start=(k == 0), stop=(k == nk - 1))
                return ins
            P.op("pe", fn, reads=bscr[tt][k0:k0 + nk] + [bCONST], writes=[bk[1]])
            i = rstd_from_bank(bk, n, inv_n)
            for k in range(nk):
                eng = "dve"
                P.op(eng, lambda e, k=k: e.scalar_tensor_tensor(
                    out=dst[:, k0 + k, c0:c0 + n], in0=src[:, k0 + k, c0:c0 + n], scalar=PV[:, gcol + k:gcol + k + 1],
                    in1=RS[:, i, 0:n], op0=ALU.mult, op1=ALU.mult),
                    reads=[bsrc[tt][k0 + k], bRS[i], bCONST], writes=[bdst[tt][k0 + k]])

        def evac_add_x(obase):
            def f(o, tt, bk, bb):
                c0, n = TTS[tt]
                P.op("dve", lambda e: e.tensor_tensor(out=XT[:, obase + o, c0:c0 + n], in0=bk[:, 0:n],
                                                      in1=XT[:, obase + o, c0:c0 + n], op=ALU.add),
                     reads=[bb], writes=[bXT[tt][obase + o]])
            return f

        def transpose_out(src_fn, np_in, nfree, dt):
            pass

        cv0 = Carve()
        XST = cv0.take([2, 1024], F32)
        bXST = [Buf(reg), Buf(reg)]
        xsem = [P.newsem("xin0"), P.newsem("xin1")]
        def load_x_tile(c):
            np_ = 128 if c < 16 else NS
            sl = c % 2
            src = xp[c * 128:(c + 1) * 128, :] if c < 16 else xs[:, :]
            P.op("sp", lambda e, sl=sl, np_=np_, src=src: e.dma_start(out=XST[0:np_, sl, :], in_=src),
                 writes=[bXST[sl]], sem=xsem[sl])
            tt = c // 4 if c < 16 else 4
            col = c * 128
            for half in range(2):
                bk = P.bank()

                def fn(e, bk=bk, sl=sl, np_=np_, half=half):
                    ins = None
                    for q in range(4):
                        f = half * 4 + q
                        ins = e.transpose(out=bk[0][:, q * 128:q * 128 + np_], in_=XST[0:np_, sl, f * 128:(f + 1) * 128],
                                          identity=IDF[0:np_, 0:np_])
                    return ins
                P.op("pe", fn, reads=[bXST[sl], bCONST], writes=[bk[1]])
                eng = "act" if half == 0 else "dve"
                if eng == "act":
                    P.op("act", lambda e, bk=bk, half=half, col=col, np_=np_: e.activation(
                        out=XT[:, half * 4:half * 4 + 4, col:col + np_],
                        in_=bk[0][:, :].rearrange("p (q t) -> p q t", q=4)[:, :, 0:np_], func=AF.Copy),
                        reads=[bk[1]], writes=bXT[tt][half * 4:half * 4 + 4])
                else:
                    P.op("dve", lambda e, bk=bk, half=half, col=col, np_=np_: e.tensor_copy(
                        out=XT[:, half * 4:half * 4 + 4, col:col + np_],
                        in_=bk[0][:, :].rearrange("p (q t) -> p q t", q=4)[:, :, 0:np_]),
                        reads=[bk[1]], writes=bXT[tt][half * 4:half * 4 + 4])

        for c in range(17):
            load_x_tile(c)
            P.cut(0.5 + c * 0.01)
        P.cut(1)

        def layer(l):
            pb = l * PL
            reg.switch()
            cm = Carve()
            MA = cm.take([2, PAD + NCOL], BF16); MB = cm.take([2, PAD + NCOL], BF16); MC = cm.take([2, PAD + NCOL], BF16)
            DG = cm.take([8, 128], BF16)
            BCT = cm.take([512], F32)
            WSTB = cm.take([4, 128], BF16); WMT = cm.take([4, 128], BF16); MASKB = cm.take([128], BF16)
            PWB = cm.take([2, 128], BF16)
            BSROW = cm.take([4, 128], BF16)
            ONEROW = cm.take([128], BF16)
            VT = cm.take([2, 256], F32)
            VN = cm.take([4, 256], BF16)
            VNS = cm.take([256], F32)
            ST = cm.take([2, NS * 30], F32); STP = cm.take([2, NS * 15], F32); STS = cm.take([2, NS * 2], F32)
            SMP = cm.take([8, NS], F32)
            TOUT = cm.take([2, 256], F32)
            R = lambda: Buf(reg)
            bMA = [[R(), R()] for _ in TTS]; bMB = [[R(), R()] for _ in TTS]; bMC = [[R(), R()] for _ in TTS]
            bDG = [R() for _ in range(8)]
            bPRM = R(); bWMT = R(); bVT = [R() for _ in range(2)]; bVN = [R() for _ in range(4)]; bVNS = R()
            bST = R(); bSTP = R(); bSTS = R(); bSMP = [R() for _ in range(8)]; bTOUT = [R(), R()]
            bPADS = R()
            dgi = [0]
            touti = [0]

            P.op("sp", lambda e: e.dma_start(out=BCT[:], in_=bcd[l]), writes=[bPRM], sem="ldm")
            P.op("pool", lambda e: e.dma_start(out=WSTB[:], in_=wstd[l].rearrange("p (h t) -> p h t", h=4)), writes=[bPRM], sem="pq")
            P.op("pool", lambda e: e.dma_start(out=MASKB[:], in_=maskd[:, :]), writes=[bPRM], sem="pq")
            P.op("pool", lambda e: e.dma_start(out=PWB[:], in_=pwbd[l].rearrange("p (j c) -> p j c", j=2)), writes=[bPRM], sem="pq")
            P.op("pool", lambda e: e.dma_start(out=BSROW[0:1], in_=bsrd[l].rearrange("o (h t) -> o h t", h=4)), writes=[bPRM], sem="pq")
            P.op("dve", lambda e: e.memset(ONEROW[0:1, :], 1.0), writes=[bPRM])
            for h in range(4):
                P.op("dve", lambda e, h=h: e.tensor_tensor(out=WMT[:, h, :], in0=WSTB[:, h, :], in1=MASKB[:, :], op=ALU.mult),
                     reads=[bPRM], writes=[bWMT])
            for M_ in (MA, MB, MC):
                P.op("dve", lambda e, M_=M_: e.memset(M_[:, :, 0:PAD], 0.0), writes=[bPADS])
            ldsem = "ldm"
            for (srcd, rows_per, dstT, bdst, ncol) in ((sglu, 120, ST, bST, NS * 30), (spl, 120, STP, bSTP, NS * 15),
                                                      (ssh, 32, STS, bSTS, NS * 2)):
                ntile = ncol // rows_per
                for i in range(ntile):
                    ti = nxt("tmp")
                    P.op("sp", lambda e, ti=ti, i=i, srcd=srcd, rows_per=rows_per: e.dma_start(
                        out=TMP[0:rows_per, ti, 0:256], in_=srcd[l, i * rows_per:(i + 1) * rows_per, :]),
                        writes=[bTMP[ti]], sem=tmpsem[ti])
                    bk = P.bank()

                    def fn(e, bk=bk, ti=ti, rows_per=rows_per):
                        ins = None
                        for j in range(2):
                            ins = e.transpose(out=bk[0][:, j * 128:j * 128 + rows_per], in_=TMP[0:rows_per, ti, j * 128:(j + 1) * 128],
                                              identity=IDF[0:rows_per, 0:rows_per])
                        return ins
                    P.op("pe", fn, reads=[bTMP[ti], bCONST], writes=[bk[1]])
                    P.op("dve", lambda e, bk=bk, i=i, rows_per=rows_per, dstT=dstT: e.tensor_copy(
                        out=dstT[:, :, i * rows_per:(i + 1) * rows_per],
                        in_=bk[0][:, 0:256].rearrange("p (j r) -> p j r", j=2)[:, :, 0:rows_per]),
                        reads=[bk[1]], writes=[bdst])

            rmsnorm(XT, bXT, 0, 8, pb + 0, HT, bHT, HT, bHT, 0, 1.0 / 1024)

            P.cut(15 * l + 2)

            def new_diag(col):
                i = dgi[0]
                dgi[0] = (i + 1) % 8
                P.op("dve", lambda e: e.tensor_scalar(out=DG[:, i, :], in0=IDB[:, :], scalar1=PV[:, col:col + 1], scalar2=None,
                                                      op0=ALU.mult),
                     reads=[bCONST], writes=[bDG[i]])
                return i

            def out_tokmajor(src_fn, ncols, dst_ap, bsrc):
                oi = touti[0]
                touti[0] = (oi + 1) % 2
                bk = P.bank()
                bkb = bk[0][:, :].bitcast(BF16)

                def fn(e):
                    ins = None
                    for j in range(2):
                        ins = e.transpose(out=bkb[0:ncols, j * 128:(j + 1) * 128], in_=src_fn(j), identity=IDB[:, :])
                    return ins
                P.op("pe", fn, reads=bsrc + [bCONST], writes=[bk[1]])
                P.op("dve", lambda e: e.tensor_copy(out=TOUT[0:ncols, oi, :], in_=bkb[0:ncols, 0:256]),
                     reads=[bk[1]], writes=[bTOUT[oi]])
                P.op("sp", lambda e: e.dma_start(out=dst_ap, in_=TOUT[0:ncols, oi, :]), reads=[bTOUT[oi]], sem=toutsem[oi])

            s0 = load_block(w_in[l, :, 0:512])
            s1 = load_block(w_in[l, :, 512:1024])

            def evac_u(o, tt, bk, bb):
                c0, n = TTS[tt]
                P.op("act", lambda e: e.activation(out=MA[:, o, PAD + c0:PAD + c0 + n], in_=bk[:, 0:n], func=AF.Copy),
                     reads=[bb], writes=[bMA[tt][o]])
            def gmlp_tile(tt, mid):
                c0, n = TTS[tt]
                proj_block(s0, HT, bHT, evac_u, tts=[tt], otiles=range(2))
                nch = 4 if tt < 4 else 1
                np_ = 128 if tt < 4 else NS
                vb = [P.bank() for _ in range((nch + 1) // 2)]

                def fnv(e, vb=vb, c0=c0, nch=nch, np_=np_):
                    ins = None
                    for c in range(nch):
                        for k in range(8):
                            ins = e.matmul(vb[c // 2][0][0:np_, (c % 2) * 256:(c % 2) * 256 + 256],
                                           lhsT=HT[:, k, c0 + c * 128:c0 + c * 128 + np_], rhs=RING[:, s0, k, 256:512],
                                           start=(k == 0), stop=(k == 7))
                    return ins
                P.op("pe", fnv, reads=[bRING[s0]] + bHT[tt], writes=[b[1] for b in vb])
                mv, bmv = stat_cols(8 + 24)
                for c in range(nch):
                    vsrc = vb[c // 2][0][0:np_, (c % 2) * 256:(c % 2) * 256 + 256]
                    P.op("dve", lambda e, vsrc=vsrc, c=c: e.bn_stats(out=mv[0:np_, 8 + 6 * c:8 + 6 * c + 6], in_=vsrc),
                         reads=[vb[c // 2][1]], writes=bmv)
                    P.op("dve", lambda e, c=c: e.bn_aggr(out=mv[0:np_, 2 * c:2 * c + 2], in_=mv[0:np_, 8 + 6 * c:8 + 6 * c + 6]),
                         reads=bmv, writes=bmv)
                mv3 = mv[0:np_, 0:2 * nch].rearrange("p (c t) -> p c t", t=2)
                P.op("act", lambda e: e.activation(out=mv3[:, :, 1:2], in_=mv3[:, :, 1:2], func=AF.Ln, bias=EPST[0:np_, :]),
                     reads=bmv + [bCONST], writes=bmv)
                P.op("act", lambda e: e.activation(out=mv3[:, :, 1:2], in_=mv3[:, :, 1:2], func=AF.Exp, scale=-0.5),
                     reads=bmv, writes=bmv)
                scr3 = mv[0:np_, 8:8 + 6 * nch].rearrange("p (c s) -> p c s", s=6)
                P.op("dve", lambda e: e.scalar_tensor_tensor(out=scr3[:, :, 0:1], in0=mv3[:, :, 0:1], scalar=-1.0, in1=mv3[:, :, 1:2],
                                                             op0=ALU.mult, op1=ALU.mult), reads=bmv, writes=bmv)
                for c in range(nch):
                    vsrc = vb[c // 2][0][0:np_, (c % 2) * 256:(c % 2) * 256 + 256]
                    P.op("act", lambda e, vsrc=vsrc, c=c: e.activation(
                        out=VT[0:np_, c % 2, :], in_=vsrc, func=AF.Identity, scale=mv[0:np_, 2 * c + 1:2 * c + 2],
                        bias=mv[0:np_, 8 + 6 * c:8 + 6 * c + 1]), reads=[vb[c // 2][1]] + bmv, writes=[bVT[c % 2]])
                    P.op("dve", lambda e, c=c: e.tensor_tensor(out=VT[0:np_, c % 2, :], in0=VT[0:np_, c % 2, :], in1=BCT[0:np_, 0:256], op=ALU.mult),
                         reads=[bVT[c % 2], bPRM], writes=[bVT[c % 2]])
                    if tt < 4:
                        P.op("dve", lambda e, c=c: e.tensor_tensor(out=VN[:, c, :], in0=VT[:, c % 2, :], in1=BCT[:, 256:512], op=ALU.add),
                             reads=[bVT[c % 2], bPRM], writes=[bVN[c]])
                    else:
                        P.op("dve", lambda e: e.tensor_tensor(out=VNS[0:NS, :], in0=VT[0:NS, 0, :], in1=BCT[0:NS, 256:512], op=ALU.add),
                             reads=[bVT[0], bPRM], writes=[bVNS])
                mid()
                if tt < 4:
                    zb = [P.bank() for _ in range(4)]

                    def fnz(e, zb=zb):
                        ins = None
                        for c in range(4):
                            for pr in range(2):
                                for ee in range(2):
                                    h = 2 * pr + ee
                                    outp = zb[pr * 2 + ee][0][:, c * 128:(c + 1) * 128]
                                    e.matmul(outp, lhsT=ONEROW[0:1, :], rhs=BSROW[0:1, h, :], start=True, stop=False)
                                    ins = e.matmul(outp, lhsT=VN[:, c, pr * 128:(pr + 1) * 128], rhs=WMT[:, h, :], start=False, stop=True)
                        return ins
                    P.op("pe", fnz, reads=bVN + [bWMT, bPRM], writes=[b[1] for b in zb])
                    for pr in range(2):
                        for ee in range(2):
                            zz = zb[pr * 2 + ee]
                            P.op("dve", lambda e, zz=zz, pr=pr, ee=ee, c0=c0: e.tensor_tensor(
                                out=GT[ee * 64:(ee + 1) * 64, pr, c0:c0 + 512], in0=zz[0][ee * 64:(ee + 1) * 64, :],
                                in1=MA[ee * 64:(ee + 1) * 64, pr, PAD + c0:PAD + c0 + 512], op=ALU.mult),
                                reads=[zz[1], bMA[tt][pr]], writes=[bGT[tt][pr]])
                else:
                    P.op("sp", lambda e: e.dma_start(out=o_gv[l], in_=VNS[0:NS, :]), reads=[bVNS], sem="gvs")
                    bk = P.bank()

                    def fnt(e, bk=bk):
                        ins = None
                        for j in range(2):
                            ins = e.transpose(out=bk[0][:, j * NS:(j + 1) * NS], in_=VNS[0:NS, j * 128:(j + 1) * 128], identity=IDF[0:NS, 0:NS])
                        return ins
                    P.op("pe", fnt, reads=[bVNS, bCONST], writes=[bk[1]])
                    for j in range(2):
                        P.op("dve", lambda e, bk=bk, j=j: e.tensor_scalar(
                            out=SMP[:, j, :], in0=bk[0][:, j * NS:(j + 1) * NS], scalar1=PV[:, pb + 48 + j:pb + 49 + j],
                            scalar2=PV[:, pb + 50 + j:pb + 51 + j], op0=ALU.mult, op1=ALU.add),
                            reads=[bk[1], bCONST], writes=[bSMP[j]])
                        P.op("dve", lambda e, j=j: e.tensor_tensor(out=GT[:, j, NT:NCOL], in0=SMP[:, j, :], in1=MA[:, j, PAD + NT:PAD + NCOL], op=ALU.mult),
                             reads=[bSMP[j], bMA[4][j]], writes=[bGT[4][j]])

            def evac_glu(tt, bks):
                c0, n = TTS[tt]
                for j in range(2):
                    P.op("act", lambda e, j=j: e.activation(out=MB[:, j, PAD + c0:PAD + c0 + n], in_=bks[j][0][:, 0:n], func=AF.Copy),
                         reads=[bks[j][1]], writes=[bMB[tt][j]])
                    P.op("act", lambda e, j=j: e.activation(out=MC[:, j, PAD + c0:PAD + c0 + n], in_=bks[2 + j][0][:, 0:n], func=AF.Copy),
                         reads=[bks[2 + j][1]], writes=[bMC[tt][j]])
            def glu_tile(tt):
                c0, n = TTS[tt]
                bks = [P.bank() for _ in range(4)]

                def fn(e):
                    ins = None
                    for o in range(4):
                        for k in range(8):
                            ins = e.matmul(bks[o][0][:, 0:n], lhsT=RING[:, s1, k, o * 128:(o + 1) * 128], rhs=HT[:, k, c0:c0 + n],
                                           start=(k == 0), stop=(k == 7))
                    return ins
                P.op("pe", fn, reads=[bRING[s1]] + bHT[tt], writes=[b[1] for b in bks])
                evac_glu(tt, bks)
            for tt in range(5):
                if tt + 1 < 5:
                    rmsnorm(XT, bXT, 0, 8, pb + 0, HT, bHT, HT, bHT, tt + 1, 1.0 / 1024)
                gmlp_tile(tt, lambda tt=tt: glu_tile(tt))

            def glu_finish(tt):
                c0, n = TTS[tt]
                for j in range(2):
                    P.op("act", lambda e, j=j: e.activation(out=MC[:, j, PAD + c0:PAD + c0 + n], in_=MC[:, j, PAD + c0:PAD + c0 + n], func=AF.Sigmoid),
                         reads=[bMC[tt][j]], writes=[bMC[tt][j]])
                    P.op("dve", lambda e, j=j: e.tensor_tensor(out=MB[:, j, PAD + c0:PAD + c0 + n], in0=MB[:, j, PAD + c0:PAD + c0 + n],
                                                               in1=MC[:, j, PAD + c0:PAD + c0 + n], op=ALU.mult),
                         reads=[bMC[tt][j]], writes=[bMB[tt][j]])
            for tt in range(5):
                glu_finish(tt)

            P.cut(15 * l + 3)
            s2 = load_block(w_in[l, :, 1024:1536])

            out_tokmajor(lambda j: MB[:, j, PAD + NT - 30:PAD + NT], 30, o_glp[l], [bMB[3][0], bMB[3][1]])
            P.op("sp", lambda e: e.dma_start(out=o_gls[l].rearrange("(b k) c -> b k c", k=30)[:, 0:29, :],
                                             in_=sglu[l].rearrange("(b k) c -> b k c", k=30)[:, 1:30, :]), sem="outs")
            out_tokmajor(lambda j: MB[:, j, PAD + NT:PAD + NCOL], NS, o_gls[l].rearrange("(b k) c -> b k c", k=30)[:, 29, :],
                         [bMB[4][0], bMB[4][1]])

            P.cut(15 * l + 4)

            def conv_prompt(src, bsrc, ntap, wcol, tapoff, evac):
                for j in range(2):
                    bks = [P.bank() for _ in range(4)]
                    for k in range(ntap):
                        di_ = new_diag(wcol(j, k))

                        def fn(e, di_=di_, k=k, j=j, bks=bks):
                            ins = None
                            for tt in range(4):
                                c0 = TTS[tt][0]
                                a = PAD + c0 + tapoff(k)
                                ins = e.matmul(bks[tt][0][:, :], lhsT=DG[:, di_, :], rhs=src[:, j, a:a + 512],
                                               start=(k == 0), stop=(k == ntap - 1))
                            return ins
                        rd = [bDG[di_], bPADS] + [bsrc[tt][j] for tt in range(4)]
                        P.op("pe", fn, reads=rd, writes=[b[1] for b in bks])
                    for tt in range(4):
                        evac(j, tt, bks[tt])

            def evac_cb(j, tt, bk):
                c0, n = TTS[tt]
                P.op("act", lambda e: e.activation(out=GT[:, 2 + j, c0:c0 + n], in_=bk[0][:, 0:n], func=AF.Identity,
                                                   bias=PV[:, pb + 40 + j:pb + 41 + j]),
                     reads=[bk[1], bCONST], writes=[bGT[tt][2 + j]])
                P.op("act", lambda e: e.activation(out=GT[:, 6 + j, c0:c0 + n], in_=bk[0][:, 0:n], func=AF.Square,
                                                   bias=PV[:, pb + 40 + j:pb + 41 + j]),
                     reads=[bk[1], bCONST], writes=[bGT[tt][6 + j]])

            s3 = load_block(w_in[l, :, 1536:2048])

            def evac_b2(o, tt, bk, bb):
                c0, n = TTS[tt]
                if o < 2:
                    P.op("act", lambda e: e.activation(out=MA[:, o, PAD + c0:PAD + c0 + n], in_=bk[:, 0:n], func=AF.Copy),
                         reads=[bb], writes=[bMA[tt][o]])
                else:
                    P.op("act", lambda e: e.activation(out=MC[:, o - 2, PAD + c0:PAD + c0 + n], in_=bk[:, 0:n], func=AF.Copy),
                         reads=[bb], writes=[bMC[tt][o - 2]])
            proj_block(s2, HT, bHT, evac_b2)

            conv_prompt(MB, bMB, 31, lambda j, k: pb + 52 + j * 31 + k, lambda k: k - 30, evac_cb)
            for j in range(2):
                ti = nxt("tmp")
                P.op("dve", lambda e, j=j, ti=ti: e.tensor_tensor(
                    out=TMP[:, ti, 0:NS * 30].rearrange("p (b k) -> p b k", k=30),
                    in0=ST[:, j, :].rearrange("p (b k) -> p b k", k=30),
                    in1=PV[:, pb + 52 + j * 31:pb + 52 + j * 31 + 30].unsqueeze(1).broadcast_to([128, NS, 30]), op=ALU.mult),
                    reads=[bST, bCONST], writes=[bTMP[ti]])
                P.op("dve", lambda e, j=j, ti=ti: e.tensor_reduce(out=SMP[:, 2 + j, :], in_=TMP[:, ti, 0:NS * 30].rearrange("p (b k) -> p b k", k=30),
                                                                  axis=AX.X, op=ALU.add),
                     reads=[bTMP[ti]], writes=[bSMP[2 + j]])
                P.op("dve", lambda e, j=j: e.scalar_tensor_tensor(out=SMP[:, 2 + j, :], in0=MB[:, j, PAD + NT:PAD + NCOL],
                                                                  scalar=PV[:, pb + 52 + j * 31 + 30:pb + 52 + j * 31 + 31],
                                                                  in1=SMP[:, 2 + j, :], op0=ALU.mult, op1=ALU.add),
                     reads=[bMB[4][j], bSMP[2 + j], bCONST], writes=[bSMP[2 + j]])
                P.op("act", lambda e, j=j: e.activation(out=GT[:, 2 + j, NT:NCOL], in_=SMP[:, 2 + j, :], func=AF.Identity,
                                                        bias=PV[:, pb + 40 + j:pb + 41 + j]),
                     reads=[bSMP[2 + j], bCONST], writes=[bGT[4][2 + j]])
                P.op("act", lambda e, j=j: e.activation(out=GT[:, 6 + j, NT:NCOL], in_=SMP[:, 2 + j, :], func=AF.Square,
                                                        bias=PV[:, pb + 40 + j:pb + 41 + j]),
                     reads=[bSMP[2 + j], bCONST], writes=[bGT[4][6 + j]])

            P.cut(15 * l + 5)
            def evac_b3(o, tt, bk, bb):
                c0, n = TTS[tt]
                if o < 2:
                    P.op("dve", lambda e: e.tensor_tensor(out=MC[:, o, PAD + c0:PAD + c0 + n], in0=bk[:, 0:n],
                                                          in1=MC[:, o, PAD + c0:PAD + c0 + n], op=ALU.mult),
                         reads=[bb], writes=[bMC[tt][o]])
                else:
                    P.op("act", lambda e: e.activation(out=MB[:, o - 2, PAD + c0:PAD + c0 + n], in_=bk[:, 0:n], func=AF.Copy),
                         reads=[bb], writes=[bMB[tt][o - 2]])

            def conf_ln_b(tt, j, c0, n):
                P.op("act", lambda e: e.activation(out=GT[:, 6 + j, c0:c0 + n], in_=GT[:, 2 + j, c0:c0 + n], func=AF.Sigmoid,
                                                                   scale=PV[:, pb + 42 + j:pb + 43 + j], bias=PV[:, pb + 44 + j:pb + 45 + j]),
                     reads=[bGT[tt][2 + j], bCONST], writes=[bGT[tt][6 + j]])
                P.op("dve", lambda e: e.tensor_scalar(out=GT[:, 2 + j, c0:c0 + n], in0=GT[:, 2 + j, c0:c0 + n],
                                                                      scalar1=PV[:, pb + 42 + j:pb + 43 + j], scalar2=PV[:, pb + 44 + j:pb + 45 + j],
                                                                      op0=ALU.mult, op1=ALU.add),
                     reads=[bGT[tt][6 + j], bCONST], writes=[bGT[tt][2 + j]])
                P.op("dve", lambda e: e.tensor_tensor(out=GT[:, 2 + j, c0:c0 + n], in0=GT[:, 2 + j, c0:c0 + n],
                                                                      in1=GT[:, 6 + j, c0:c0 + n], op=ALU.mult),
                     reads=[bGT[tt][2 + j], bGT[tt][6 + j]], writes=[bGT[tt][2 + j]])

            def conf_ln_tile(tt, part):
                c0, n = TTS[tt]
                if part == 1:
                    for j in range(2):
                        conf_ln_b(tt, j, c0, n)
                    return
                bm = P.bank(); bq = P.bank()

                def fn(e, bm=bm, bq=bq, c0=c0, n=n):
                    ins = None
                    for j in range(2):
                        e.matmul(bm[0][:, 0:n], lhsT=ONESB[:, :], rhs=GT[:, 2 + j, c0:c0 + n], start=(j == 0), stop=(j == 1))
                    for j in range(2):
                        ins = e.matmul(bq[0][:, 0:n], lhsT=ONESB[:, :], rhs=GT[:, 6 + j, c0:c0 + n], start=(j == 0), stop=(j == 1))
                    return ins
                P.op("pe", fn, reads=bGT[tt][2:4] + bGT[tt][6:8] + [bCONST], writes=[bm[1], bq[1]])
                t0 = nxt("tmp")
                P.op("dve", lambda e, t0=t0, bm=bm, n=n: e.tensor_scalar(out=TMP[:, t0, 0:n], in0=bm[0][:, 0:n], scalar1=1.0 / 256, scalar2=None,
                                                                      op0=ALU.mult), reads=[bm[1]], writes=[bTMP[t0]])
                t1 = nxt("tmp")
                P.op("dve", lambda e, t0=t0, t1=t1, n=n: e.tensor_tensor(out=TMP[:, t1, 0:n], in0=TMP[:, t0, 0:n], in1=TMP[:, t0, 0:n], op=ALU.mult),
                     reads=[bTMP[t0]], writes=[bTMP[t1]])
                P.op("dve", lambda e, t1=t1, bq=bq, n=n: e.scalar_tensor_tensor(out=TMP[:, t1, 0:n], in0=bq[0][:, 0:n], scalar=1.0 / 256,
                                                                               in1=TMP[:, t1, 0:n], op0=ALU.mult, op1=ALU.subtract),
                     reads=[bq[1], bTMP[t1]], writes=[bTMP[t1]])
                ri = nxt("rs")
                P.op("act", lambda e, t1=t1, ri=ri, n=n: e.activation(out=RS[:, ri, 0:n], in_=TMP[:, t1, 0:n], func=AF.Ln, bias=EPST[:, :]),
                     reads=[bTMP[t1], bCONST], writes=[bRS[ri]])
                P.op("act", lambda e, ri=ri, n=n: e.activation(out=RS[:, ri, 0:n], in_=RS[:, ri, 0:n], func=AF.Exp, scale=-0.5),
                     reads=[bRS[ri]], writes=[bRS[ri]])
                for j in range(2):
                    P.op("dve", lambda e, j=j, t0=t0, c0=c0, n=n: e.tensor_tensor(out=GT[:, 2 + j, c0:c0 + n], in0=GT[:, 2 + j, c0:c0 + n],
                                                                                 in1=TMP[:, t0, 0:n], op=ALU.subtract),
                         reads=[bTMP[t0]], writes=[bGT[tt][2 + j]])
                    P.op("dve", lambda e, j=j, ri=ri, c0=c0, n=n: e.tensor_tensor(out=GT[:, 2 + j, c0:c0 + n], in0=GT[:, 2 + j, c0:c0 + n],
                                                                                 in1=RS[:, ri, 0:n], op=ALU.mult),
                         reads=[bRS[ri]], writes=[bGT[tt][2 + j]])
            for tt in range(5):
                conf_ln_tile(tt, 0)
                proj_block(s3, HT, bHT, evac_b3, tts=[tt])

            P.cut(15 * l + 6)
            out_tokmajor(lambda j: MC[:, j, PAD + NT - 2:PAD + NT], 2, o_shp[l], [bMC[3][0], bMC[3][1]])
            P.op("sp", lambda e: e.dma_start(out=o_shs[l].rearrange("(b k) c -> b k c", k=2)[:, 0, :],
                                             in_=ssh[l].rearrange("(b k) c -> b k c", k=2)[:, 1, :]), sem="outs")
            out_tokmajor(lambda j: MC[:, j, PAD + NT:PAD + NCOL], NS, o_shs[l].rearrange("(b k) c -> b k c", k=2)[:, 1, :],
                         [bMC[4][0], bMC[4][1]])

            def evac_c(j, tt, bk):
                c0, n = TTS[tt]
                P.op("dve", lambda e: e.tensor_tensor(out=GT[:, 4 + j, c0:c0 + n], in0=bk[0][:, 0:n], in1=MA[:, j, PAD + c0:PAD + c0 + n], op=ALU.mult),
                     reads=[bk[1], bMA[tt][j]], writes=[bGT[tt][4 + j]])
            conv_prompt(MC, bMC, 3, lambda j, k: pb + 114 + j * 3 + k, lambda k: k - 2, evac_c)
            for j in range(2):
                sts = STS[:, j, :].rearrange("p (b k) -> p b k", k=2)
                P.op("dve", lambda e, j=j, sts=sts: e.tensor_scalar(out=SMP[:, 4 + j, :], in0=sts[:, :, 0], scalar1=PV[:, pb + 114 + j * 3:pb + 115 + j * 3],
                                                                    scalar2=None, op0=ALU.mult), reads=[bSTS, bCONST], writes=[bSMP[4 + j]])
                P.op("dve", lambda e, j=j, sts=sts: e.scalar_tensor_tensor(out=SMP[:, 4 + j, :], in0=sts[:, :, 1], scalar=PV[:, pb + 115 + j * 3:pb + 116 + j * 3],
                                                                           in1=SMP[:, 4 + j, :], op0=ALU.mult, op1=ALU.add),
                     reads=[bSTS, bSMP[4 + j], bCONST], writes=[bSMP[4 + j]])
                P.op("dve", lambda e, j=j: e.scalar_tensor_tensor(out=SMP[:, 4 + j, :], in0=MC[:, j, PAD + NT:PAD + NCOL], scalar=PV[:, pb + 116 + j * 3:pb + 117 + j * 3],
                                                                  in1=SMP[:, 4 + j, :], op0=ALU.mult, op1=ALU.add),
                     reads=[bMC[4][j], bSMP[4 + j], bCONST], writes=[bSMP[4 + j]])
                P.op("dve", lambda e, j=j: e.tensor_tensor(out=GT[:, 4 + j, NT:NCOL], in0=SMP[:, 4 + j, :], in1=MA[:, j, PAD + NT:PAD + NCOL], op=ALU.mult),
                     reads=[bSMP[4 + j], bMA[4][j]], writes=[bGT[4][4 + j]])

            P.cut(15 * l + 7)
            out_tokmajor(lambda j: MB[:, j, PAD + NT - 15:PAD + NT], 15, o_plp[l], [bMB[3][0], bMB[3][1]])
            P.op("sp", lambda e: e.dma_start(out=o_pls[l].rearrange("(b k) c -> b k c", k=15)[:, 0:14, :],
                                             in_=spl[l].rearrange("(b k) c -> b k c", k=15)[:, 1:15, :]), sem="outs")
            out_tokmajor(lambda j: MB[:, j, PAD + NT:PAD + NCOL], NS, o_pls[l].rearrange("(b k) c -> b k c", k=15)[:, 14, :],
                         [bMB[4][0], bMB[4][1]])
            DV = 2 * PL + 8
            RT = 2 * PL + 40

            def evac_pool(j, tt, bk):
                c0, n = TTS[tt]
                P.op("act", lambda e: e.activation(out=MC[:, j, PAD + c0:PAD + c0 + n], in_=bk[0][:, 0:n], func=AF.Copy),
                     reads=[bk[1]], writes=[bMC[tt][j]])
                if tt == 0:
                    P.op("dve", lambda e: e.tensor_tensor(out=SMP[:, 6 + j, 0:15], in0=bk[0][:, 0:15], in1=MB[:, j, PAD:PAD + 15], op=ALU.add),
                         reads=[bk[1], bMB[0][j]], writes=[bSMP[6 + j]])
                    P.op("dve", lambda e: e.tensor_tensor(out=SMP[:, 6 + j, 0:15], in0=SMP[:, 6 + j, 0:15], in1=PV[:, RT + j * 16:RT + j * 16 + 15], op=ALU.mult),
                         reads=[bSMP[6 + j], bCONST], writes=[bSMP[6 + j]])
                    P.op("dve", lambda e: e.tensor_tensor(out=MC[:, j, PAD:PAD + 15], in0=SMP[:, 6 + j, 0:15], in1=MB[:, j, PAD:PAD + 15], op=ALU.subtract),
                         reads=[bSMP[6 + j], bMB[0][j]], writes=[bMC[0][j]])
            conv_prompt(MB, bMB, 16, lambda j, k: DV + j * 16 + (15 - k), lambda k: k - 15, evac_pool)
            for j in range(2):
                ti = nxt("tmp")
                P.op("dve", lambda e, j=j, ti=ti: e.tensor_tensor(
                    out=TMP[:, ti, 0:NS * 15].rearrange("p (b k) -> p b k", k=15),
                    in0=STP[:, j, :].rearrange("p (b k) -> p b k", k=15),
                    in1=PV[:, RT + 32 + j * 16:RT + 32 + j * 16 + 15].unsqueeze(1).broadcast_to([128, NS, 15]), op=ALU.mult),
                    reads=[bSTP, bCONST], writes=[bTMP[ti]])
                P.op("dve", lambda e, j=j, ti=ti: e.tensor_reduce(out=SMP[:, 6 + j, :], in_=TMP[:, ti, 0:NS * 15].rearrange("p (b k) -> p b k", k=15),
                                                                  axis=AX.X, op=ALU.add), reads=[bTMP[ti]], writes=[bSMP[6 + j]])
                P.op("dve", lambda e, j=j: e.scalar_tensor_tensor(out=MC[:, j, PAD + NT:PAD + NCOL], in0=MB[:, j, PAD + NT:PAD + NCOL], scalar=PV[:, DV + j * 16:DV + j * 16 + 1],
                                                                  in1=SMP[:, 6 + j, :], op0=ALU.mult, op1=ALU.add),
                     reads=[bMB[4][j], bSMP[6 + j], bCONST], writes=[bMC[4][j]])
            for tt in range(5):
                conf_ln_tile(tt, 1)
            for tt in range(5):
                c0, n = TTS[tt]
                for j in range(2):
                    bk = P.bank()
                    P.op("pe", lambda e, bk=bk, j=j, c0=c0, n=n: e.matmul(bk[0][:, 0:n], lhsT=PWB[:, j, :], rhs=MC[:, j, PAD + c0:PAD + c0 + n], start=True, stop=True),
                         reads=[bPRM, bMC[tt][j]], writes=[bk[1]])
                    P.op("act", lambda e, bk=bk, j=j, c0=c0, n=n: e.activation(out=GT[:, 6 + j, c0:c0 + n], in_=bk[0][:, 0:n], func=AF.Copy,
                                                                              scale=PV[:, pb + 46 + j:pb + 47 + j]),
                         reads=[bk[1], bCONST], writes=[bGT[tt][6 + j]])

            P.cut(15 * l + 8)
            reg.switch()
            ca = Carve()
            MEMTOK = ca.take([1024], F32)
            MNB = ca.take([1024], BF16)
            SEL = ca.take([NS, 128], BF16)
            MTB = ca.take([8, 256], BF16)
            KT = ca.take([8, 256], BF16)
            VTOK = ca.take([2, 1024], BF16)
            PB_ = ca.take([2, 1024], BF16)
            PTS = ca.take([2, 1024], BF16)
            KS = ca.take([3, 2048], BF16)
            VS = KS
            QTOK = ca.take([1024], BF16)
            SS = ca.take([128], F32)
            PSM = ca.take([256], BF16)
            PTS2 = ca.take([2, 64], BF16)
            JUNK = ca.take([4, 256], BF16)
            bJUNK = [R() for _ in range(4)]
            jki = [0]
            bMEMTOK = [R()] * 2; bMNB = [R()] * 2; bMTB = R(); bSEL = R(); bKT = R(); bVTOK = R()
            bPB = [R(), R()]; bPTS = [R(), R()]; bKS = [R(), R(), R()]; bVS = bKS; bQTOK = R(); bSS = R(); bPSM = R(); bPTS2 = R()
            kssem = [P.newsem("ks0_%d" % l), P.newsem("ks1_%d" % l), P.newsem("ks2_%d" % l)]
            vssem = kssem
            msem = [P.newsem("mem0_%d" % l)] * 2

            so0 = load_block(w_out[l, :, 0:512])

            def gn(tt):
                for g in range(4):
                    rmsnorm(GT, bGT, 2 * g, 2, pb + 32 + 2 * g, GT, bGT, HT, bHT, tt, 1.0 / 256)

            P.cut(15 * l + 9)
            P.op("dve", lambda e: e.tensor_copy(out=SEL[0:NS, :, :], in_=IDB[0:NS, 0:NS].unsqueeze(2).broadcast_to([NS, NS, 128])), reads=[bCONST], writes=[bSEL])

            P.cut(15 * l + 9.1)

            def mem_tile(mt):
                P.op("sp", lambda e, mt=mt: e.dma_start(out=MEMTOK[:, :], in_=mem[mt * 128:(mt + 1) * 128, :]), writes=[bMEMTOK[mt]], sem=msem[mt])
                sc, bsc = stat_cols(2)
                P.op("dve", lambda e, sc=sc: e.memset(sc[:, 0:1], 0.0), writes=bsc)
                ti = nxt("tmp")
                P.op("dve", lambda e, mt=mt, sc=sc, ti=ti: e.scalar_tensor_tensor(out=TMP[:, ti, 0:512], in0=MEMTOK[:, 0:512], scalar=1.0, in1=MEMTOK[:, 0:512],
                                                                                  op0=ALU.mult, op1=ALU.mult, accum_out=sc[:, 0:1]),
                     reads=[bMEMTOK[mt]] + bsc, writes=[bTMP[ti]] + bsc)
                P.op("dve", lambda e, sc=sc: e.memset(sc[:, 1:2], 0.0), writes=bsc)
                ti = nxt("tmp")
                P.op("dve", lambda e, mt=mt, sc=sc, ti=ti: e.scalar_tensor_tensor(out=TMP[:, ti, 0:512], in0=MEMTOK[:, 512:1024], scalar=1.0, in1=MEMTOK[:, 512:1024],
                                                                                  op0=ALU.mult, op1=ALU.mult, accum_out=sc[:, 1:2]),
                     reads=[bMEMTOK[mt]] + bsc, writes=[bTMP[ti]] + bsc)
                P.op("dve", lambda e, sc=sc: e.tensor_tensor(out=sc[:, 0:1], in0=sc[:, 0:1], in1=sc[:, 1:2], op=ALU.add), reads=bsc, writes=bsc)
                P.op("act", lambda e, sc=sc: e.activation(out=sc[:, 0:1], in_=sc[:, 0:1], func=AF.Ln, bias=EPST[:, :], scale=1.0 / 1024), reads=bsc + [bCONST], writes=bsc)
                P.op("act", lambda e, sc=sc: e.activation(out=sc[:, 0:1], in_=sc[:, 0:1], func=AF.Exp, scale=-0.5), reads=bsc, writes=bsc)
                P.op("dve", lambda e, mt=mt, sc=sc: e.tensor_scalar(out=MNB[:, :], in0=MEMTOK[:, :], scalar1=sc[:, 0:1], scalar2=None, op0=ALU.mult),
                     reads=[bMEMTOK[mt]] + bsc, writes=[bMNB[mt]])
                for half in range(2):
                    bk = P.bank()
                    bkb = bk[0][:, :].bitcast(BF16)

                    def fn(e, bkb=bkb, mt=mt, half=half):
                        ins = None
                        for q in range(4):
                            f = half * 4 + q
                            ins = e.transpose(out=bkb[:, q * 128:(q + 1) * 128], in_=MNB[:, f * 128:(f + 1) * 128], identity=IDB[:, :])
                        return ins
                    P.op("pe", fn, reads=[bMNB[mt], bCONST], writes=[bk[1]])
                    for q in range(4):
                        f = half * 4 + q
                        P.op("dve", lambda e, bkb=bkb, q=q, f=f, mt=mt: e.tensor_scalar(out=MTB[:, f, mt * 128:(mt + 1) * 128], in0=bkb[:, q * 128:(q + 1) * 128],
                                                                                       scalar1=PV[:, pb + 24 + f:pb + 25 + f], scalar2=None, op0=ALU.mult),
                             reads=[bk[1], bCONST], writes=[bMTB])

            gn(0)
            mem_tile(0)
            for tt in range(5):
                if tt + 1 < 5:
                    gn(tt + 1)
                if tt == 1:
                    mem_tile(1)
                proj_block(so0, GT, bGT, evac_add_x(0), tts=[tt])
            so1 = load_block(w_out[l, :, 512:1024])
            proj_block(so1, GT, bGT, evac_add_x(4))
            P.cut(15 * l + 9.3)

            def kv_proj(slots, odram, dstT, bdstT, dsttok, bdsttok):
                for hb, s in enumerate(slots):
                    for mt in range(2):
                        bk = P.bank()

                        def fn(e, bk=bk, s=s, mt=mt):
                            ins = None
                            for k in range(8):
                                ins = e.matmul(bk[0][:, :], lhsT=MTB[:, k, mt * 128:(mt + 1) * 128], rhs=RING[:, s, k, :], start=(k == 0), stop=(k == 7))
                            return ins
                        P.op("pe", fn, reads=[bMTB, bRING[s]], writes=[bk[1]])
                        ti = nxt("tmp")
                        P.op("act", lambda e, bk=bk, ti=ti: e.activation(out=TMP[:, ti, :], in_=bk[0][:, :], func=AF.Copy), reads=[bk[1]], writes=[bTMP[ti]])
                        P.op("sp", lambda e, ti=ti, mt=mt, hb=hb: e.dma_start(out=odram[l, mt * 128:(mt + 1) * 128, hb * 512:(hb + 1) * 512], in_=TMP[:, ti, :]),
                             reads=[bTMP[ti]], sem=tmpsem[ti])
                        if dsttok is not None:
                            P.op("dve", lambda e, bk=bk, mt=mt, hb=hb: e.tensor_copy(out=dsttok[:, mt, hb * 512:(hb + 1) * 512], in_=bk[0][:, :]),
                                 reads=[bk[1]], writes=[bdsttok])
                    if dstT is not None:
                        for o in range(4):
                            bk = P.bank()

                            def fn2(e, bk=bk, s=s, o=o):
                                ins = None
                                for k in range(8):
                                    ins = e.matmul(bk[0][:, 0:256], lhsT=RING[:, s, k, o * 128:(o + 1) * 128], rhs=MTB[:, k, :], start=(k == 0), stop=(k == 7))
                                return ins
                            P.op("pe", fn2, reads=[bMTB, bRING[s]], writes=[bk[1]])
                            P.op("act", lambda e, bk=bk, o=o, hb=hb: e.activation(out=dstT[:, hb * 4 + o, :], in_=bk[0][:, 0:256], func=AF.Copy), reads=[bk[1]], writes=[bdstT])
            sk0 = load_block(w_xk[l, :, 0:512])
            sk1 = load_block(w_xk[l, :, 512:1024])
            kv_proj([sk0, sk1], o_mk, KT, bKT, None, None)
            P.cut(15 * l + 9.5)
            sv0 = load_block(w_xv[l, :, 0:512])
            sv1 = load_block(w_xv[l, :, 512:1024])
            kv_proj([sv0, sv1], o_mv, None, None, VTOK, bVTOK)

            P.cut(15 * l + 10)
            def evac_q(obase):
                def f(o, tt, bk, bb):
                    c0, n = TTS[tt]
                    P.op("act", lambda e: e.activation(out=GT[:, obase + o, c0:c0 + n], in_=bk[:, 0:n], func=AF.Copy, scale=1.0 / 16),
                         reads=[bb], writes=[bGT[tt][obase + o]])
                return f
            sq0 = load_block(w_xq[l, :, 0:512])
            rmsnorm(XT, bXT, 0, 8, pb + 8, HT, bHT, HT, bHT, 0, 1.0 / 1024)
            for tt in range(5):
                if tt + 1 < 5:
                    rmsnorm(XT, bXT, 0, 8, pb + 8, HT, bHT, HT, bHT, tt + 1, 1.0 / 1024)
                proj_block(sq0, HT, bHT, evac_q(0), tts=[tt])
            sq1 = load_block(w_xq[l, :, 512:1024])
            proj_block(sq1, HT, bHT, evac_q(4))
            so0 = load_block(w_xo[l, :, 0:512])

            P.cut(15 * l + 11)
            qb = P.bank()
            qbb = qb[0][:, :].bitcast(BF16)

            def fnq(e):
                ins = None
                for o in range(8):
                    ins = e.transpose(out=qbb[0:NS, o * 128:(o + 1) * 128], in_=GT[:, o, NT:NCOL], identity=IDB[:, :])
                return ins
            P.op("pe", fnq, reads=bGT[4] + [bCONST], writes=[qb[1]])
            P.op("act", lambda e: e.activation(out=QTOK[0:NS, :], in_=qbb[0:NS, :], func=AF.Copy), reads=[qb[1]], writes=[bQTOK])
            bSSc = [R() for _ in range(128)]
            KST = {}

            def samp_k_pre(b):
                sl = b % 3
                P.op("pool", lambda e: e.dma_start(out=KS[:, sl, :].rearrange("p (t c) -> p t c", t=2), in_=ck[l, b].rearrange("(t p) c -> p t c", p=128)),
                     writes=[bKS[sl]], sem=kssem[sl])

            def samp_k_half(b, half):
                sl = b % 3
                if b == 0 and half == 0:
                    P.op("dve", lambda e: e.memset(SS[:, :], 0.0), writes=bSSc)
                qh = P.bank()
                P.op("pe", lambda e: e.matmul(qh[0][:, :], lhsT=SEL[0:NS, b, :], rhs=QTOK[0:NS, half * 512:(half + 1) * 512], start=True, stop=True),
                     reads=[bQTOK, bSEL], writes=[qh[1]])
                for mt in range(2):
                    for hl in range(2):
                        h = 2 * half + hl
                        col = mt * 64 + b * 4 + h
                        jk = jki[0]
                        jki[0] = (jk + 1) % 4
                        P.op("dve", lambda e, mt=mt, h=h, hl=hl, col=col, jk=jk: e.scalar_tensor_tensor(
                            out=JUNK[:, jk, :], in0=KS[:, sl, mt * 1024 + h * 256:mt * 1024 + h * 256 + 256], scalar=1.0,
                            in1=qh[0][:, hl * 256:hl * 256 + 256], op0=ALU.mult, op1=ALU.mult, accum_out=SS[:, col:col + 1]),
                            reads=[bKS[sl], qh[1]], writes=[bSSc[col], bJUNK[jk]])

            AST = {}

            def attn_A(ti_):
                tt = ti_ // 4
                c0 = ti_ * 128
                sb0 = P.bank(); i0 = P.last_idx; P.reserved.add(i0)
                sb1 = P.bank(); i1 = P.last_idx; P.reserved.add(i1)
                sbs = [sb0, sb1]
                AST[ti_] = (sbs, i0, i1)

                def fns(e):
                    ins = None
                    for h in range(4):
                        for dh in range(2):
                            ins = e.matmul(sbs[h // 2][0][:, (h % 2) * 256:(h % 2) * 256 + 256], lhsT=GT[:, 2 * h + dh, c0:c0 + 128], rhs=KT[:, 2 * h + dh, :],
                                           start=(dh == 0), stop=(dh == 1))
                    return ins
                P.op("pe", fns, reads=bGT[tt] + [bKT], writes=[sb0[1], sb1[1]])

            BST = {}

            def attn_B1(ti_):
                sl = ti_ % 2
                sbs, i0, i1 = AST[ti_]
                P.reserved.discard(i0); P.reserved.discard(i1)
                mx, bmx = stat_cols(12)
                bm = [bmx[0]]
                bs_ = [bmx[1]]
                for hp in range(2):
                    P.op("dve", lambda e, hp=hp: e.tensor_reduce(out=mx[:, 2 * hp:2 * hp + 2], in_=sbs[hp][0][:, :].rearrange("p (h m) -> p h m", h=2),
                                                                 axis=AX.X, op=ALU.max, negate=True), reads=[sbs[hp][1]], writes=bm)
                for h in range(4):
                    P.op("act", lambda e, h=h: e.activation(out=PB_[:, sl, h * 256:(h + 1) * 256], in_=sbs[h // 2][0][:, (h % 2) * 256:(h % 2) * 256 + 256],
                                                            func=AF.Exp, bias=mx[:, h:h + 1], accum_out=mx[:, 8 + h:9 + h]),
                         reads=[sbs[h // 2][1]] + bm, writes=[bPB[sl]] + bs_)
                BST[ti_] = (mx, bm, bs_)

            def attn_Bn(ti_):
                sl = ti_ % 2
                mx, bm, bs_ = BST[ti_]
                P.op("dve", lambda e: e.reciprocal(out=mx[:, 4:8], in_=mx[:, 8:12]), reads=bs_, writes=bm)
                P.op("dve", lambda e: e.tensor_tensor(out=PB_[:, sl, :].rearrange("p (h m) -> p h m", h=4), in0=PB_[:, sl, :].rearrange("p (h m) -> p h m", h=4),
                                                      in1=mx[:, 4:8].unsqueeze(2).broadcast_to([128, 4, 256]), op=ALU.mult),
                     reads=[bPB[sl]] + bm, writes=[bPB[sl]])

            def attn_B(ti_):
                tt = ti_ // 4
                c0 = ti_ * 128
                sl = ti_ % 2
                mx, bm, bs_ = BST[ti_]
                samp_k_half(ti_, 0)
                samp_k_half(ti_, 1)
                tb = P.bank()
                tbb = tb[0][:, :].bitcast(BF16)

                def fnt(e):
                    ins = None
                    for q in range(8):
                        ins = e.transpose(out=tbb[:, q * 128:(q + 1) * 128], in_=PB_[:, sl, q * 128:(q + 1) * 128], identity=IDB[:, :])
                    return ins
                P.op("pe", fnt, reads=[bPB[sl], bCONST], writes=[tb[1]])
                P.op("act", lambda e: e.activation(out=PTS[:, sl, :], in_=tbb[:, :], func=AF.Copy), reads=[tb[1]], writes=[bPTS[sl]])
                ob0 = P.bank(); ob1 = P.bank()
                obs = [ob0, ob1]

                def fno(e):
                    ins = None
                    for h in range(4):
                        for dh in range(2):
                            q = 2 * h + dh
                            for mt in range(2):
                                ins = e.matmul(obs[q // 4][0][:, (q % 4) * 128:(q % 4) * 128 + 128], lhsT=VTOK[:, mt, h * 256 + dh * 128:h * 256 + dh * 128 + 128],
                                               rhs=PTS[:, sl, (2 * h + mt) * 128:(2 * h + mt) * 128 + 128], start=(mt == 0), stop=(mt == 1))
                    return ins
                P.op("pe", fno, reads=[bVTOK, bPTS[sl]], writes=[ob0[1], ob1[1]])
                for hh in range(2):
                    P.op("act", lambda e, hh=hh: e.activation(out=HT[:, hh * 4:hh * 4 + 4, c0:c0 + 128], in_=obs[hh][0][:, :].rearrange("p (q t) -> p q t", q=4), func=AF.Copy),
                         reads=[obs[hh][1]], writes=bHT[tt][hh * 4:hh * 4 + 4])

            attn_A(0)
            attn_A(1)
            samp_k_pre(0)
            attn_B1(0)
            for ti_ in range(16):
                if ti_ + 2 < 16:
                    attn_A(ti_ + 2)
                attn_Bn(ti_)
                if ti_ + 1 < 16:
                    samp_k_pre(ti_ + 1)
                    attn_B1(ti_ + 1)
                attn_B(ti_)
            P.cut(15 * l + 12)

            stb = P.bank()

            def fnst(e):
                ins = None
                for mt in range(2):
                    ins = e.transpose(out=stb[0][0:64, mt * 128:(mt + 1) * 128], in_=SS[:, mt * 64:(mt + 1) * 64], identity=IDF[:, :])
                return ins
            P.op("pe", fnst, reads=bSSc + [bCONST], writes=[stb[1]])
            mx, bmx = stat_cols(4)
            P.op("dve", lambda e: e.tensor_reduce(out=mx[0:64, 1:2], in_=stb[0][0:64, 0:256], axis=AX.X, op=ALU.max, negate=True), reads=[stb[1]], writes=bmx)
            P.op("act", lambda e: e.activation(out=PSM[0:64, :], in_=stb[0][0:64, 0:256], func=AF.Exp, bias=mx[0:64, 1:2], accum_out=mx[0:64, 2:3]),
                 reads=[stb[1]] + bmx, writes=[bPSM] + bmx)
            P.op("dve", lambda e: e.reciprocal(out=mx[0:64, 3:4], in_=mx[0:64, 2:3]), reads=bmx, writes=bmx)
            P.op("dve", lambda e: e.tensor_scalar(out=PSM[0:64, :], in0=PSM[0:64, :], scalar1=mx[0:64, 3:4], scalar2=None, op0=ALU.mult), reads=[bPSM] + bmx, writes=[bPSM])
            ptb = P.bank()
            ptbb = ptb[0][:, :].bitcast(BF16)

            def fnpt(e):
                ins = None
                for mt in range(2):
                    ins = e.transpose(out=ptbb[:, mt * 64:(mt + 1) * 64], in_=PSM[0:64, mt * 128:(mt + 1) * 128], identity=IDB[0:64, 0:64])
                return ins
            P.op("pe", fnpt, reads=[bPSM, bCONST], writes=[ptb[1]])
            P.op("act", lambda e: e.activation(out=PTS2[:, :, :], in_=ptbb[:, 0:128].rearrange("p (t c) -> p t c", t=2), func=AF.Copy), reads=[ptb[1]], writes=[bPTS2])
            osb = P.bank()
            osb_idx = P.last_idx
            P.reserved.add(osb_idx)

            def samp_v(b):
                sl = b % 3
                P.op("pool", lambda e, b=b, sl=sl: e.dma_start(out=VS[:, sl, :].rearrange("p (t c) -> p t c", t=2), in_=cv[l, b].rearrange("(t p) c -> p t c", p=128)),
                     writes=[bVS[sl]], sem=vssem[sl])

                def fnv2(e, b=b, sl=sl):
                    ins = None
                    for h in range(4):
                        for dh in range(2):
                            q = 2 * h + dh
                            for mt in range(2):
                                ins = e.matmul(osb[0][:, q * NS + b:q * NS + b + 1], lhsT=VS[:, sl, mt * 1024 + h * 256 + dh * 128:mt * 1024 + h * 256 + dh * 128 + 128],
                                               rhs=PTS2[:, mt, b * 4 + h:b * 4 + h + 1], start=(mt == 0), stop=(mt == 1))
                    return ins
                P.op("pe", fnv2, reads=[bVS[sl], bPTS2], writes=[osb[1]])

            P.cut(15 * l + 13)
            so1 = load_block(w_xo[l, :, 512:1024])
            for tt in range(4):
                proj_block(so0, HT, bHT, evac_add_x(0), tts=[tt])
                samp_v(2 * tt); samp_v(2 * tt + 1)
            for tt in range(4):
                proj_block(so1, HT, bHT, evac_add_x(4), tts=[tt])
                samp_v(8 + 2 * tt); samp_v(9 + 2 * tt)
            P.op("act", lambda e: e.activation(out=HT[:, :, NT:NCOL], in_=osb[0][:, 0:8 * NS].rearrange("p (q b) -> p q b", q=8), func=AF.Copy),
                 reads=[osb[1]], writes=bHT[4])
            P.reserved.discard(osb_idx)
            proj_block(so0, HT, bHT, evac_add_x(0), tts=[4])
            proj_block(so1, HT, bHT, evac_add_x(4), tts=[4])

            P.cut(15 * l + 14)
            reg.switch()

            def evac_ff1(obase):
                def f(o, tt, bk, bb):
                    c0, n = TTS[tt]
                    ti = nxt("tmp")
                    P.op("act", lambda e: e.activation(out=TMP[:, ti, 0:n], in_=bk[:, 0:n], func=AF.Relu), reads=[bb], writes=[bTMP[ti]])
                    P.op("dve", lambda e: e.tensor_tensor(out=GT[:, obase + o, c0:c0 + n], in0=TMP[:, ti, 0:n], in1=TMP[:, ti, 0:n], op=ALU.mult),
                         reads=[bTMP[ti]], writes=[bGT[tt][obase + o]])
                return f
            for c in range(4):
                sa = load_block(w_ff1[l, :, c * 1024:c * 1024 + 512])
                if c == 0:
                    rmsnorm(XT, bXT, 0, 8, pb + 16, HT, bHT, HT, bHT, 0, 1.0 / 1024)
                    for tt in range(5):
                        if tt + 1 < 5:
                            rmsnorm(XT, bXT, 0, 8, pb + 16, HT, bHT, HT, bHT, tt + 1, 1.0 / 1024)
                        proj_block(sa, HT, bHT, evac_ff1(0), tts=[tt])
                else:
                    proj_block(sa, HT, bHT, evac_ff1(0))
                sb_ = load_block(w_ff1[l, :, c * 1024 + 512:c * 1024 + 1024])
                proj_block(sb_, HT, bHT, evac_ff1(4))
                sc_ = load_block(w_ff2[l, c * 1024:(c + 1) * 1024, 0:512])
                proj_block(sc_, GT, bGT, evac_add_x(0))
                sd_ = load_block(w_ff2[l, c * 1024:(c + 1) * 1024, 512:1024])
                proj_block(sd_, GT, bGT, evac_add_x(4))

        for l in range(L):
            layer(l)
            P.cut(15 * l + 15)

        reg.switch()
        cf = Carve()
        YF2 = cf.take([2, 8, 512], F32)
        OST = cf.take([2, 1024], F32)
        bYF2 = [[Buf(reg) for _ in range(8)] for _ in range(2)]
        bOST = [Buf(reg), Buf(reg)]
        ostsem = [P.newsem("ost0"), P.newsem("ost1")]
        FG = 2 * PL
        oi_ = [0]

        def final_tile(tt):
            c0, n = TTS[tt]
            YF = YF2[:, tt % 2]
            bYF = bYF2[tt % 2]
            P.op("act", lambda e, c0=c0, n=n: e.activation(out=HT[:, :, c0:c0 + n], in_=XT[:, :, c0:c0 + n], func=AF.Square), reads=bXT[tt], writes=bHT[tt])
            bk = P.bank()

            def fn(e, bk=bk, c0=c0, n=n):
                ins = None
                for k in range(8):
                    ins = e.matmul(bk[0][:, 0:n], lhsT=ONESB[:, :], rhs=HT[:, k, c0:c0 + n], start=(k == 0), stop=(k == 7))
                return ins
            P.op("pe", fn, reads=bHT[tt] + [bCONST], writes=[bk[1]])
            ri = rstd_from_bank(bk, n, 1.0 / 1024)
            for k in range(8):
                P.op("dve", lambda e, k=k, ri=ri, c0=c0, n=n: e.scalar_tensor_tensor(out=YF[:, k, 0:n], in0=XT[:, k, c0:c0 + n], scalar=PV[:, FG + k:FG + k + 1],
                                                                                    in1=RS[:, ri, 0:n], op0=ALU.mult, op1=ALU.mult),
                     reads=[bXT[tt][k], bRS[ri], bCONST], writes=[bYF[k]])
            nch = 4 if tt < 4 else 1
            np_ = 128 if tt < 4 else NS
            for c in range(nch):
                sl = oi_[0] % 2
                oi_[0] += 1
                for half in range(2):
                    bk2 = P.bank()

                    def fn2(e, bk2=bk2, c=c, half=half, np_=np_):
                        ins = None
                        for q in range(4):
                            f = half * 4 + q
                            ins = e.transpose(out=bk2[0][0:np_, q * 128:(q + 1) * 128], in_=YF[:, f, c * 128:c * 128 + np_], identity=IDF[:, :])
                        return ins
                    P.op("pe", fn2, reads=bYF[half * 4:half * 4 + 4] + [bCONST], writes=[bk2[1]])
                    if half == 0:
                        P.op("act", lambda e, bk2=bk2, sl=sl, np_=np_: e.activation(out=OST[0:np_, sl, 0:512], in_=bk2[0][0:np_, :], func=AF.Copy), reads=[bk2[1]], writes=[bOST[sl]])
                    else:
                        P.op("dve", lambda e, bk2=bk2, sl=sl, np_=np_: e.tensor_copy(out=OST[0:np_, sl, 512:1024], in_=bk2[0][0:np_, :]), reads=[bk2[1]], writes=[bOST[sl]])
                dst = o_yp[c0 + c * 128:c0 + (c + 1) * 128, :] if tt < 4 else o_ys[:, :]
                P.op("sp", lambda e, sl=sl, np_=np_, dst=dst: e.dma_start(out=dst, in_=OST[0:np_, sl, :]), reads=[bOST[sl]], sem=ostsem[sl])

        for tt in range(5):
            final_tile(tt)

        with nc.Block() as block:
            @block.tensor
            def _(e):
                P.emit("pe", e)

            @block.scalar
            def _(e):
                P.emit("act", e)

            @block.vector
            def _(e):
                P.emit("dve", e)

            @block.gpsimd
            def _(e):
                P.emit("pool", e)

            @block.sync
            def _(e):
                P.emit("sp", e)
                P.final_waits(e)
    return nc


_NC = None


def _fm(v):
    v = np.asarray(v, np.float32)
    return np.ascontiguousarray(v.reshape(-1, 128).T)


def _host_params(inp):
    pv = np.zeros((128, NPV), np.float32)
    for l in range(L):
        b = l * PL
        pv[:, b + 0:b + 8] = _fm(inp["norm_mix"][l])
        pv[:, b + 8:b + 16] = _fm(inp["norm_xattn"][l])
        pv[:, b + 16:b + 24] = _fm(inp["norm_ffn"][l])
        pv[:, b + 24:b + 32] = _fm(inp["norm_mem"][l])
        pv[:, b + 32:b + 40] = _fm(inp["mix_out_g"][l])
        pv[:, b + 40:b + 42] = _fm(inp["conf_dw_b"][l])
        pv[:, b + 42:b + 44] = _fm(inp["conf_ln_g"][l])
        pv[:, b + 44:b + 46] = _fm(inp["conf_ln_b"][l])
        pv[:, b + 46:b + 48] = _fm(inp["pool_scale"][l])
        pv[:, b + 48:b + 50] = _fm(np.repeat(np.asarray(inp["gmlp_ws"])[l, :, 0, 0], 64))
        pv[:, b + 50:b + 52] = _fm(np.repeat(np.asarray(inp["gmlp_bs"])[l, :, 0], 64))
        cdw = np.asarray(inp["conf_dw"])[l]
        for j in range(2):
            pv[:, b + 52 + j * 31:b + 52 + (j + 1) * 31] = cdw[:, j * 128:(j + 1) * 128].T
        sdw = np.asarray(inp["sc_dw"])[l]
        for j in range(2):
            pv[:, b + 114 + j * 3:b + 114 + (j + 1) * 3] = sdw[:, j * 128:(j + 1) * 128].T
    g = 2 * PL
    pv[:, g:g + 8] = _fm(inp["norm_final"])
    wins = np.repeat(np.array([2, 4, 8, 16], np.float32), 64)
    for j in range(2):
        w = wins[j * 128:(j + 1) * 128]
        for k in range(16):
            pv[:, g + 8 + j * 16 + k] = np.where(k < w, 1.0 / w, 0.0) - (1.0 if k == 0 else 0.0)
        for t in range(16):
            pv[:, g + 40 + j * 16 + t] = w / np.minimum(w, t + 1.0)
        for kk in range(15):
            k = 15 - kk
            pv[:, g + 72 + j * 16 + kk] = np.where(k < w, 1.0 / w, 0.0)
    bc = np.zeros((L, 128, 512), np.float32)
    wst = np.zeros((L, 128, 512), np.float32)
    pwb = np.zeros((L, 128, 256), np.float32)
    bsrow = np.zeros((L, 1, 512), np.float32)
    for l in range(L):
        bc[l, :, 0:256] = np.asarray(inp["gmlp_ln_g"])[l][None, :]
        bc[l, :, 256:512] = np.asarray(inp["gmlp_ln_b"])[l][None, :]
        ws = np.asarray(inp["gmlp_ws"])[l]
        wst[l] = np.transpose(ws, (2, 0, 1)).reshape(128, 512)
        pw = np.asarray(inp["pool_w"])[l]
        for j in range(2):
            for gg in range(2):
                pwb[l, gg * 64:(gg + 1) * 64, j * 128 + gg * 64:j * 128 + (gg + 1) * 64] = pw[2 * j + gg]
        bsrow[l, 0] = np.asarray(inp["gmlp_bs"])[l].reshape(512)
    mask = np.triu(np.ones((128, 128), np.float32))
    idf = np.eye(128, dtype=np.float32)
    return dict(pv=pv, bc=bc, wst=wst, pwb=pwb, bsrow=bsrow, mask=mask, idf=idf)


def kernel(**inp):
    global _NC
    inp = {k: np.asarray(v) for k, v in inp.items()}
    if _NC is None:
        _NC = build_nc()
    nc = _NC
    hp = _host_params(inp)
    shared = {k: np.ascontiguousarray(inp[k], dtype=np.float32) for k in
              ("w_in", "w_out", "w_xq", "w_xk", "w_xv", "w_xo", "w_ff1", "w_ff2")}
    shared.update(hp)
    in_maps = []
    for c in range(8):
        b0 = c * NS
        m = dict(shared)
        m["xp"] = np.ascontiguousarray(inp["x_prompt"][c])
        m["xs"] = np.ascontiguousarray(inp["x_sample"][b0:b0 + NS, 0, :])
        m["mem"] = np.ascontiguousarray(inp["mem_prompt"][c])
        m["ck"] = np.ascontiguousarray(inp["cache_mem_k"][:, b0:b0 + NS].reshape(L, NS, 256, 1024))
        m["cv"] = np.ascontiguousarray(inp["cache_mem_v"][:, b0:b0 + NS].reshape(L, NS, 256, 1024))
        m["sglu"] = np.ascontiguousarray(inp["state_conv_glu"][:, b0:b0 + NS].reshape(L, NS * 30, 256))
        m["ssh"] = np.ascontiguousarray(inp["state_conv_short"][:, b0:b0 + NS].reshape(L, NS * 2, 256))
        m["spl"] = np.ascontiguousarray(inp["state_pool"][:, b0:b0 + NS].reshape(L, NS * 15, 256))
        in_maps.append(m)
    res = run_bass_kernel_spmd(nc, in_maps, core_ids=list(range(8)))
    R = res.results
    cat = lambda k, ax: np.concatenate([np.asarray(r[k]) for r in R], axis=ax)
    y_prompt = np.stack([np.asarray(r["o_yp"]) for r in R], 0).astype(np.float32)
    y_sample = cat("o_ys", 0).reshape(128, 1, 1024).astype(np.float32)
    mk = np.stack([np.asarray(r["o_mk"]) for r in R], 1).reshape(L, 8, 256, 4, 256).astype(np.float32)
    mv = np.stack([np.asarray(r["o_mv"]) for r in R], 1).reshape(L, 8, 256, 4, 256).astype(np.float32)
    glp = np.stack([np.asarray(r["o_glp"]) for r in R], 1).astype(np.float32)
    gls = np.concatenate([np.asarray(r["o_gls"]).reshape(L, NS, 30, 256) for r in R], 1).astype(np.float32)
    shp = np.stack([np.asarray(r["o_shp"]) for r in R], 1).astype(np.float32)
    shs = np.concatenate([np.asarray(r["o_shs"]).reshape(L, NS, 2, 256) for r in R], 1).astype(np.float32)
    plp = np.stack([np.asarray(r["o_plp"]) for r in R], 1).astype(np.float32)
    pls = np.concatenate([np.asarray(r["o_pls"]).reshape(L, NS, 15, 256) for r in R], 1).astype(np.float32)
    gv = np.concatenate([np.asarray(r["o_gv"]).reshape(L, NS, 1, 256) for r in R], 1).astype(np.float32)
    return (y_prompt, y_sample, mk, mv, glp, gls, shp, shs, plp, pls, gv)
```
